# Optimizing a Trainium2 kernel written in Bass

```python
import math
import jax, jax.numpy as jnp
from jax import lax
import numpy as np

D_MODEL = 1024
BATCH = 32
SEQ = 2048
DEPTH = 1
DEC_BATCH = 32
DEC_SEQ = 32
PAST_LEN = 2048

CHUNK = 64
Q_BLOCK = 128
H_A = 8
D_NOPE = 64
D_ROPE = 32
D_VA = 64
D_CQ = 256
D_CKV = 256
ROPE_BASE = 10000.0
H_B = 4
D_HB = 64
W_B = H_B * 2 * D_HB
D_MIX = H_A * D_VA + W_B
D_IN = D_CQ + D_CKV + D_ROPE + 3 * W_B
D_FF = 2816
CONV_W = 3
D_PLE = 256
ALPHA = (2 * DEPTH) ** 0.25
BETA = (8 * DEPTH) ** -0.25
EPS = 1e-5

kernel_name = "hymba_mla_diffattn_convglu_stream_step"


def _rms(x, g):
    xf = x.astype(jnp.float32)
    y = xf * lax.rsqrt(jnp.mean(xf * xf, -1, keepdims=True) + EPS)
    return (y * g.astype(jnp.float32)).astype(x.dtype)


def _ln(x, g, b):
    xf = x.astype(jnp.float32)
    mu = jnp.mean(xf, -1, keepdims=True)
    var = jnp.mean(jnp.square(xf - mu), -1, keepdims=True)
    y = (xf - mu) * lax.rsqrt(var + EPS)
    return (y * g.astype(jnp.float32) + b.astype(jnp.float32)).astype(x.dtype)


def _rope(x, pos):
    half = D_ROPE // 2
    inv = 1.0 / (ROPE_BASE ** (jnp.arange(half, dtype=jnp.float32) / half))
    ang = pos.astype(jnp.float32)[:, None] * inv[None, :]
    shape = (1, pos.shape[0]) + (1,) * (x.ndim - 3) + (half,)
    cos = jnp.cos(ang).reshape(shape)
    sin = jnp.sin(ang).reshape(shape)
    xf = x.astype(jnp.float32)
    x1, x2 = xf[..., :half], xf[..., half:]
    return jnp.concatenate([x1 * cos - x2 * sin, x1 * sin + x2 * cos], -1).astype(x.dtype)


def _alibi_slopes(n):
    return np.array([2.0 ** (-8.0 * (h + 1) / n) for h in range(n)], dtype=np.float32)


def _attend(q, k, v, q_pos, k_pos, slopes, coef, scale):
    b, sq, h, m, dk = q.shape
    bs = Q_BLOCK if sq % Q_BLOCK == 0 else sq
    nb = sq // bs
    k_chunk = k_pos // CHUNK
    coef32 = coef.astype(jnp.float32)

    def block(args):
        qb, pb = args
        s = jnp.einsum('bqhmd,bkhmd->bhmqk', qb, k).astype(jnp.float32) * scale
        if slopes is not None:
            dist = jnp.abs(pb[:, None] - k_pos[None, :]).astype(jnp.float32)
            s = s - slopes[None, :, None, None, None] * dist
        allowed = k_chunk[None, :] <= (pb // CHUNK)[:, None]
        s = jnp.where(allowed, s, -jnp.inf)
        p = jax.nn.softmax(s, axis=-1)
        w = jnp.einsum('bhmqk,hm->bhqk', p, coef32).astype(v.dtype)
        return jnp.einsum('bhqk,bkhd->bqhd', w, v)

    qs = jnp.moveaxis(q.reshape(b, nb, bs, h, m, dk), 1, 0)
    ps = q_pos.reshape(nb, bs)
    o = lax.map(block, (qs, ps))
    return jnp.moveaxis(o, 0, 1).reshape(b, sq, h, v.shape[-1])


def _token_mixers(x, pos, past, lp, layer_idx):
    b, s, _ = x.shape
    proj = x @ lp['w_in']
    o1 = D_CQ
    o2 = o1 + D_CKV
    o3 = o2 + D_ROPE
    o4 = o3 + W_B
    o5 = o4 + W_B
    c_q = _rms(proj[..., :o1], lp['g_cq'])
    c_kv = _rms(proj[..., o1:o2], lp['g_ckv'])
    k_r = _rope(proj[..., o2:o3], pos)
    dq = proj[..., o3:o4].reshape(b, s, H_B, 2, D_HB)
    dk = proj[..., o4:o5].reshape(b, s, H_B, 2, D_HB)
    dv = proj[..., o5:].reshape(b, s, H_B, 2 * D_HB)
    q = (c_q @ lp['w_uq']).reshape(b, s, H_A, D_NOPE + D_ROPE)
    q = jnp.concatenate([q[..., :D_NOPE], _rope(q[..., D_NOPE:], pos)], -1)
    if past is None:
        k_pos = pos
        c_all, kr_all, dk_all, dv_all = c_kv, k_r, dk, dv
    else:
        p_ckv, p_kr, p_dk, p_dv = past
        k_pos = jnp.arange(p_ckv.shape[1] + s)
        c_all = jnp.concatenate([p_ckv, c_kv], 1)
        kr_all = jnp.concatenate([p_kr, k_r], 1)
        dk_all = jnp.concatenate([p_dk, dk], 1)
        dv_all = jnp.concatenate([p_dv, dv], 1)
    sk = c_all.shape[1]
    k_nope = (c_all @ lp['w_uk']).reshape(b, sk, H_A, D_NOPE)
    v_a = (c_all @ lp['w_uv']).reshape(b, sk, H_A, D_VA)
    k_a = jnp.concatenate([k_nope, jnp.broadcast_to(kr_all[:, :, None, :], (b, sk, H_A, D_ROPE))], -1)
    o_a = _attend(q[:, :, :, None, :], k_a[:, :, :, None, :], v_a, pos, k_pos, None,
                  jnp.ones((H_A, 1), jnp.float32), (D_NOPE + D_ROPE) ** -0.5)
    lam_init = 0.8 - 0.6 * math.exp(-0.3 * layer_idx)
    f32 = jnp.float32
    lam = (jnp.exp(jnp.sum(lp['lambda_q1'].astype(f32) * lp['lambda_k1'].astype(f32)))
           - jnp.exp(jnp.sum(lp['lambda_q2'].astype(f32) * lp['lambda_k2'].astype(f32))) + lam_init)
    coef = jnp.broadcast_to(jnp.stack([jnp.ones_like(lam), -lam]), (H_B, 2))
    slopes = jnp.asarray(_alibi_slopes(H_B))
    o_b = _attend(dq, dk_all, dv_all, pos, k_pos, slopes, coef, D_HB ** -0.5)
    o_b = _rms(o_b, lp['g_subln']) * (1.0 - lam_init)
    mix = jnp.concatenate([o_a.reshape(b, s, H_A * D_VA), o_b.reshape(b, s, W_B)], -1)
    return mix @ lp['w_o'], (c_kv, k_r, dk, dv)


def _conv_ffn(h, conv_state, lp):
    b, s, _ = h.shape
    u = h @ lp['w_up']
    if conv_state is None:
        conv_state = jnp.zeros((b, CONV_W - 1, 2 * D_FF), u.dtype)
    up = jnp.concatenate([conv_state, u], 1)
    z = lp['conv_b'] + sum(up[:, j:j + s] * lp['conv_w'][j] for j in range(CONV_W))
    g, v = z[..., :D_FF], z[..., D_FF:]
    out = (jax.nn.gelu(g, approximate=False) * v) @ lp['w_down']
    return out, up[:, -(CONV_W - 1):]


def _layer(x, p_emb, pos, past, conv_state, lp, layer_idx):
    a, rows = _token_mixers(x, pos, past, lp, layer_idx)
    h = _ln(ALPHA * x + a, lp['ln1_g'], lp['ln1_b'])
    f, new_conv = _conv_ffn(h, conv_state, lp)
    h = _ln(ALPHA * h + f, lp['ln2_g'], lp['ln2_b'])
    gate = jax.nn.sigmoid(h @ lp['w_ple_gate'] + lp['b_ple_gate'])
    y = h + gate * (p_emb @ lp['w_ple_proj'])
    return y, rows + (new_conv,)


def setup_inputs(seed: int = 0) -> dict:
    key = jax.random.key(seed)
    ks = iter(jax.random.split(key, 48))

    def nrm(shape, scale=1.0):
        return jax.random.normal(next(ks), shape, jnp.float32) * scale

    L = DEPTH
    w_in = jnp.concatenate([nrm((L, D_MODEL, D_CQ + D_CKV + D_ROPE + 2 * W_B), D_MODEL ** -0.5),
                            nrm((L, D_MODEL, W_B), BETA * D_MODEL ** -0.5)], -1)
    return {
        'x_prompt': nrm((BATCH, SEQ, D_MODEL)),
        'x_sample': nrm((DEC_BATCH, DEC_SEQ, D_MODEL)),
        'cache_ckv': nrm((L, DEC_BATCH, PAST_LEN, D_CKV)),
        'cache_krope': nrm((L, DEC_BATCH, PAST_LEN, D_ROPE)),
        'cache_diff_k': nrm((L, DEC_BATCH, PAST_LEN, H_B, 2, D_HB)),
        'cache_diff_v': nrm((L, DEC_BATCH, PAST_LEN, H_B, 2 * D_HB), BETA),
        'state_ffn_conv': nrm((L, DEC_BATCH, CONV_W - 1, 2 * D_FF)),
        'p_prompt': nrm((L, BATCH, SEQ, D_PLE)),
        'p_sample': nrm((L, DEC_BATCH, DEC_SEQ, D_PLE)),
        'w_in': w_in,
        'g_cq': 1.0 + nrm((L, D_CQ), 0.01),
        'w_uq': nrm((L, D_CQ, H_A * (D_NOPE + D_ROPE)), D_CQ ** -0.5),
        'g_ckv': 1.0 + nrm((L, D_CKV), 0.01),
        'w_uk': nrm((L, D_CKV, H_A * D_NOPE), D_CKV ** -0.5),
        'w_uv': nrm((L, D_CKV, H_A * D_VA), BETA * D_CKV ** -0.5),
        'lambda_q1': nrm((L, D_HB), 0.1),
        'lambda_k1': nrm((L, D_HB), 0.1),
        'lambda_q2': nrm((L, D_HB), 0.1),
        'lambda_k2': nrm((L, D_HB), 0.1),
        'g_subln': 1.0 + nrm((L, 2 * D_HB), 0.01),
        'w_o': nrm((L, D_MIX, D_MODEL), BETA * D_MIX ** -0.5),
        'ln1_g': 1.0 + nrm((L, D_MODEL), 0.01),
        'ln1_b': nrm((L, D_MODEL), 0.01),
        'w_up': nrm((L, D_MODEL, 2 * D_FF), D_MODEL ** -0.5),
        'conv_w': nrm((L, CONV_W, 2 * D_FF), CONV_W ** -0.5),
        'conv_b': nrm((L, 2 * D_FF), 0.01),
        'w_down': nrm((L, D_FF, D_MODEL), BETA * D_FF ** -0.5),
        'ln2_g': 1.0 + nrm((L, D_MODEL), 0.01),
        'ln2_b': nrm((L, D_MODEL), 0.01),
        'w_ple_gate': nrm((L, D_MODEL, D_MODEL), D_MODEL ** -0.5),
        'b_ple_gate': nrm((L, D_MODEL), 0.01),
        'w_ple_proj': nrm((L, D_PLE, D_MODEL), D_PLE ** -0.5),
    }


def reference(x_prompt, x_sample, cache_ckv, cache_krope, cache_diff_k, cache_diff_v, state_ffn_conv,
              p_prompt, p_sample, w_in, g_cq, w_uq, g_ckv, w_uk, w_uv, lambda_q1, lambda_k1,
              lambda_q2, lambda_k2, g_subln, w_o, ln1_g, ln1_b, w_up, conv_w, conv_b, w_down,
              ln2_g, ln2_b, w_ple_gate, b_ple_gate, w_ple_proj):
    past_len = cache_ckv.shape[2]
    pos_p = jnp.arange(x_prompt.shape[1])
    pos_s = past_len + jnp.arange(x_sample.shape[1])
    y_p, y_s = x_prompt, x_sample
    new_p, new_s = [], []
    for i in range(DEPTH):
        lp = {'w_in': w_in[i], 'g_cq': g_cq[i], 'w_uq': w_uq[i], 'g_ckv': g_ckv[i], 'w_uk': w_uk[i],
              'w_uv': w_uv[i], 'lambda_q1': lambda_q1[i], 'lambda_k1': lambda_k1[i],
              'lambda_q2': lambda_q2[i], 'lambda_k2': lambda_k2[i], 'g_subln': g_subln[i],
              'w_o': w_o[i], 'ln1_g': ln1_g[i], 'ln1_b': ln1_b[i], 'w_up': w_up[i],
              'conv_w': conv_w[i], 'conv_b': conv_b[i], 'w_down': w_down[i], 'ln2_g': ln2_g[i],
              'ln2_b': ln2_b[i], 'w_ple_gate': w_ple_gate[i], 'b_ple_gate': b_ple_gate[i],
              'w_ple_proj': w_ple_proj[i]}
        y_p, st_p = _layer(y_p, p_prompt[i], pos_p, None, None, lp, i)
        past = (cache_ckv[i], cache_krope[i], cache_diff_k[i], cache_diff_v[i])
        y_s, st_s = _layer(y_s, p_sample[i], pos_s, past, state_ffn_conv[i], lp, i)
        new_p.append(st_p)
        new_s.append(st_s)
    sp = [jnp.stack([st[j] for st in new_p]) for j in range(5)]
    ss = [jnp.stack([st[j] for st in new_s]) for j in range(5)]
    return (y_p, y_s, sp[0], sp[1], sp[2], sp[3], sp[4], ss[0], ss[1], ss[2], ss[3], ss[4])
```

```python
import math
import numpy as np
import concourse.bass as bass
import concourse.mybir as mybir
from concourse.bass_utils import run_bass_kernel_spmd

F32 = mybir.dt.float32
BF16 = mybir.dt.bfloat16
AF = mybir.ActivationFunctionType
ALU = mybir.AluOpType

D = 1024
H_A, D_NOPE, D_ROPE, D_VA, D_CQ, D_CKV = 8, 64, 32, 64, 256, 256
H_B, D_HB = 4, 64
W_B = 512
D_IN = 2080
D_FF = 2816
NCH = 22
D_PLE = 256
ALPHA = 2.0 ** 0.25
EPS = 1e-5
LAM_INIT = 0.8 - 0.6 * math.exp(0.0)
TS = 32
T = 256
SLOPES = [2.0 ** (-8.0 * (h + 1) / H_B) for h in range(H_B)]
SC_A = (D_NOPE + D_ROPE) ** -0.5
SC_B = D_HB ** -0.5
NEG = -1.0e5

import os
SAME_ENGINE_SYNC = os.environ.get('MK_SES', '1') == '1'


class Buf:
    __slots__ = ("name", "w", "r", "dsem", "dcount", "excl")

    def __init__(self, name, excl=False):
        self.name = name
        self.w = None
        self.r = {}
        self.dsem = None
        self.dcount = 0
        self.excl = excl


class FW:
    ENG = ("pe", "act", "dve", "pool", "sp")

    def __init__(self, nc):
        self.nc = nc
        self.streams = {e: [] for e in self.ENG}
        self.tick = {e: 0 for e in self.ENG}
        self.known = {e: {} for e in self.ENG}
        self.sems = {}
        self.out_tokens = []
        self.dsem_pool = []
        for e in ("pe", "act", "dve", "pool"):
            self._sem("eng_" + e)

    def _sem(self, key):
        if key not in self.sems:
            self.sems[key] = self.nc.alloc_semaphore("s_" + key)
        return key

    def _deps(self, eng, reads, writes):
        deps = {}

        def add(tok):
            if tok is None:
                return
            k, v = tok
            if deps.get(k, 0) < v:
                deps[k] = v
        writes = list(writes)
        for b in reads:
            if b.excl:
                writes.append(b)
                continue
            add(b.w)
        for b in writes:
            add(b.w)
            for k, v in b.r.items():
                add((k, v))
        own = "eng_" + eng
        out = []
        for k, v in deps.items():
            if k == own and (eng == "pe" or not SAME_ENGINE_SYNC):
                continue
            if self.known[eng].get(k, 0) >= v:
                continue
            self.known[eng][k] = v
            out.append((k, v))
        return out, writes

    def op(self, eng, fn, reads=(), writes=()):
        waits, writes = self._deps(eng, reads, writes)
        self.tick[eng] += 1
        key = "eng_" + eng
        tok = (key, self.tick[eng])
        self.streams[eng].append((waits, fn, (key, 1)))
        for b in reads:
            if not b.excl:
                b.r[key] = tok[1]
        for b in writes:
            b.w = tok
            b.r = {}
        return tok

    def dma(self, q, out_ap, in_ap, reads=(), writes=(), owner=None, final=False, **kw):
        if owner is None:
            owner = (list(writes) + list(reads))[0]
        if isinstance(owner, TT):
            owner = owner.buf
        if owner.dsem is None:
            owner.dsem = {}
            owner.dcount = {}
        kind = "sw" if q == "pool" else "hw"
        if kind not in owner.dsem:
            owner.dsem[kind] = self._sem("dma" + kind + "_" + owner.name)
            owner.dcount[kind] = 0
        waits, writes = self._deps(q, reads, writes)
        owner.dcount[kind] += 16
        tok = (owner.dsem[kind], owner.dcount[kind])

        def fn(engobj, out_ap=out_ap, in_ap=in_ap, kw=kw):
            return engobj.dma_start(out=out_ap, in_=in_ap, **kw)
        self.streams[q].append((waits, fn, (tok[0], 16)))
        for b in reads:
            if not b.excl:
                b.r[tok[0]] = tok[1]
        for b in writes:
            b.w = tok
            b.r = {}
        if final:
            self.out_tokens.append(tok)
        return tok

    def finish(self):
        fin = {}
        for k, v in self.out_tokens:
            fin[k] = max(fin.get(k, 0), v)
        self.streams["sp"].append((list(fin.items()), None, None))
        nc = self.nc
        sems = self.sems
        streams = self.streams

        def replay(eng, name):
            for waits, fn, inc in streams[name]:
                for k, v in waits:
                    eng.wait_ge(sems[k], v)
                if fn is not None:
                    ins = fn(eng)
                    ins.then_inc(sems[inc[0]], inc[1])

        with nc.Block() as block:
            @block.tensor
            def _(e):
                replay(e, "pe")

            @block.scalar
            def _(e):
                replay(e, "act")

            @block.vector
            def _(e):
                replay(e, "dve")

            @block.gpsimd
            def _(e):
                replay(e, "pool")

            @block.sync
            def _(e):
                replay(e, "sp")


class TT:
    def __init__(self, ap, buf):
        self.ap = ap
        self.buf = buf


def make_tables(S, PAST):
    nblk = max(S, PAST) // 128 + 1
    half = D_ROPE // 2
    inv = (1.0 / (10000.0 ** (np.arange(half, dtype=np.float32) / half))).astype(np.float32)
    pos = (np.arange(nblk * 128, dtype=np.float32)).reshape(nblk, 128)
    ang = pos[:, :, None] * inv[None, None, :]
    cos = np.cos(ang).astype(np.float32).transpose(1, 0, 2).copy()
    sin = np.sin(ang).astype(np.float32).transpose(1, 0, 2).copy()
    s = np.arange(128)[:, None]
    t = np.arange(128)[None, :]
    masked = (s // 64) > (t // 64)
    dt = np.zeros((5, 128, 128), np.float32)
    dt[0] = np.where(masked, 0.0, 1.0)
    for h in range(H_B):
        corr = np.where((s > t) & ((s // 64) == (t // 64)), np.exp(-2.0 * SLOPES[h] * (s - t)), 1.0)
        dt[1 + h] = np.where(masked, 0.0, corr)
    dtab = dt.transpose(1, 0, 2).copy()
    off = max(S, PAST) // 128
    nb = off + 2
    d = np.arange(nb) - off
    bt = np.zeros((128, H_B, nb), np.float32)
    for h in range(H_B):
        bt[:, h, :] = SLOPES[h] * (128.0 * d[None, :] + np.arange(128)[:, None])
    ident = np.eye(128, dtype=np.float32)
    return {"t_cos": cos, "t_sin": sin, "t_diag": dtab, "t_bias": bt, "t_ident": ident}, nblk, nb, off


def build(NP, S, NS, PAST):
    nc = bass.Bass("TRN2", target_bir_lowering=False)
    fw = FW(nc)
    tables, NBLK, NBIAS, BOFF = make_tables(S, PAST)

    def din(name, shape):
        return nc.dram_tensor(name, list(shape), F32, kind="ExternalInput").ap()

    def dout(name, shape):
        return nc.dram_tensor(name, list(shape), F32, kind="ExternalOutput").ap()

    I = {}
    I["x_p"] = din("x_p", [NP, S, D])
    I["x_s"] = din("x_s", [NS, TS, D])
    I["c_ckv"] = din("c_ckv", [NS, PAST, D_CKV])
    I["c_kr"] = din("c_kr", [NS, PAST, D_ROPE])
    I["c_dk"] = din("c_dk", [NS, PAST, W_B])
    I["c_dv"] = din("c_dv", [NS, PAST, W_B])
    I["c_conv"] = din("c_conv", [NS, 2, 2 * D_FF])
    I["p_p"] = din("p_p", [NP, S, D_PLE])
    I["p_s"] = din("p_s", [NS, TS, D_PLE])
    for nm, shp in [("w_in", [D, D_IN]), ("g_cq", [1, D_CQ]), ("w_uq", [D_CQ, 768]), ("g_ckv", [1, D_CKV]),
                    ("w_uk", [D_CKV, 512]), ("w_uv", [D_CKV, 512]), ("lq1", [1, 64]), ("lk1", [1, 64]),
                    ("lq2", [1, 64]), ("lk2", [1, 64]), ("g_sub", [1, 128]), ("w_o", [D, D]),
                    ("ln1_g", [1, D]), ("ln1_b", [1, D]), ("w_up", [D, 2 * D_FF]), ("conv_w", [3, 2 * D_FF]),
                    ("conv_b", [1, 2 * D_FF]), ("w_down", [D_FF, D]), ("ln2_g", [1, D]), ("ln2_b", [1, D]),
                    ("w_gate", [D, D]), ("b_gate", [1, D]), ("w_ple", [D_PLE, D])]:
        I[nm] = din(nm, shp)
    for nm, arr in tables.items():
        I[nm] = din(nm, arr.shape)
    O = {}
    O["y_p"] = dout("y_p", [NP, S, D])
    O["y_s"] = dout("y_s", [NS, TS, D])
    O["ckv_p"] = dout("ckv_p", [NP, S, D_CKV])
    O["kr_p"] = dout("kr_p", [NP, S, D_ROPE])
    O["dk_p"] = dout("dk_p", [NP, S, W_B])
    O["dv_p"] = dout("dv_p", [NP, S, W_B])
    O["conv_p"] = dout("conv_p", [NP, 2, 2 * D_FF])
    O["ckv_s"] = dout("ckv_s", [NS, TS, D_CKV])
    O["kr_s"] = dout("kr_s", [NS, TS, D_ROPE])
    O["dk_s"] = dout("dk_s", [NS, TS, W_B])
    O["dv_s"] = dout("dv_s", [NS, TS, W_B])
    O["conv_s"] = dout("conv_s", [NS, 2, 2 * D_FF])

    cnt = [0]

    def sb(name, shape, dt=F32, excl=False):
        cnt[0] += 1
        t = nc.alloc_sbuf_tensor(f"{name}_{cnt[0]}", list(shape), dt).ap()
        return TT(t, Buf(f"{name}_{cnt[0]}"))

    def dram(name, shape, dt=BF16):
        cnt[0] += 1
        t = nc.dram_tensor(f"{name}_{cnt[0]}", list(shape), dt).ap()
        return TT(t, Buf(f"{name}_{cnt[0]}"))

    banks = []
    for i in range(8):
        a = nc.alloc_psum_tensor(f"bank{i}", [128, 512], F32).ap()
        banks.append(TT(a, Buf(f"bank{i}", excl=True)))

    class Pool:
        def __init__(self, bs):
            self.bs = bs
            self.i = 0

        def get(self):
            b = self.bs[self.i % len(self.bs)]
            self.i += 1
            return b
    G = Pool(banks[0:4])
    V = Pool(banks[4:8])

    cp = [0]

    def copy(out_ap, in_ap, reads, writes, eng=None):
        if eng is None:
            cp[0] += 1
            eng = "act" if cp[0] % 2 else "dve"
        if eng == "act":
            fw.op("act", lambda e, o=out_ap, i=in_ap: e.copy(out=o, in_=i), reads=reads, writes=writes)
        else:
            fw.op(eng, lambda e, o=out_ap, i=in_ap: e.tensor_copy(out=o, in_=i), reads=reads, writes=writes)

    ident = sb("ident", [128, 128])
    fw.dma("sp", ident.ap, I["t_ident"], writes=[ident.buf])
    cos_t = sb("cos", [128, NBLK, 16])
    sin_t = sb("sin", [128, NBLK, 16])
    fw.dma("sp", cos_t.ap, I["t_cos"], writes=[cos_t.buf])
    fw.dma("sp", sin_t.ap, I["t_sin"], writes=[sin_t.buf])
    diag_t = sb("diag", [128, 5, 128])
    fw.dma("sp", diag_t.ap, I["t_diag"], writes=[diag_t.buf])
    bias_t = sb("bias", [128, H_B, NBIAS])
    fw.dma("sp", bias_t.ap, I["t_bias"], writes=[bias_t.buf])

    def bcast_load(name, src, n):
        t = sb(name, [128, n])
        fw.dma("sp", t.ap, src.partition_broadcast(128).rearrange("p o n -> p (o n)"), writes=[t.buf])
        return t
    g_cq = bcast_load("g_cq", I["g_cq"], D_CQ)
    g_ckv = bcast_load("g_ckv", I["g_ckv"], D_CKV)
    g_sub = bcast_load("g_sub", I["g_sub"], 128)
    ln1g = bcast_load("ln1g", I["ln1_g"], D)
    ln1b = bcast_load("ln1b", I["ln1_b"], D)
    ln2g = bcast_load("ln2g", I["ln2_g"], D)
    ln2b = bcast_load("ln2b", I["ln2_b"], D)
    lq1 = bcast_load("lq1", I["lq1"], 64)
    lk1 = bcast_load("lk1", I["lk1"], 64)
    lq2 = bcast_load("lq2", I["lq2"], 64)
    lk2 = bcast_load("lk2", I["lk2"], 64)
    eps_t = sb("eps", [128, 1])
    fw.op("dve", lambda e: e.memset(eps_t.ap, EPS), writes=[eps_t.buf])
    lam_s = sb("lam_s", [128, 8])
    fw.op("dve", lambda e: e.memset(lam_s.ap, 0.0), writes=[lam_s.buf])
    fw.op("dve", lambda e: e.tensor_tensor(out=lq1.ap, in0=lq1.ap, in1=lk1.ap, op=ALU.mult), reads=[lk1.buf], writes=[lq1.buf])
    fw.op("dve", lambda e: e.tensor_tensor(out=lq2.ap, in0=lq2.ap, in1=lk2.ap, op=ALU.mult), reads=[lk2.buf], writes=[lq2.buf])
    fw.op("act", lambda e: e.activation(out=lk1.ap, in_=lq1.ap, func=AF.Identity, accum_out=lam_s.ap[:, 0:1]), reads=[lq1.buf], writes=[lk1.buf, lam_s.buf])
    fw.op("act", lambda e: e.activation(out=lk2.ap, in_=lq2.ap, func=AF.Identity, accum_out=lam_s.ap[:, 1:2]), reads=[lq2.buf], writes=[lk2.buf, lam_s.buf])
    fw.op("act", lambda e: e.activation(out=lam_s.ap[:, 2:4], in_=lam_s.ap[:, 0:2], func=AF.Exp), reads=[], writes=[lam_s.buf])
    fw.op("dve", lambda e: e.scalar_tensor_tensor(out=lam_s.ap[:, 4:5], in0=lam_s.ap[:, 3:4], scalar=-LAM_INIT, in1=lam_s.ap[:, 2:3], op0=ALU.add, op1=ALU.subtract), writes=[lam_s.buf])
    nlam = lam_s.ap[:, 4:5]
    fw.op("dve", lambda e: e.tensor_scalar(out=g_sub.ap, in0=g_sub.ap, scalar1=1.0 - LAM_INIT, scalar2=None, op0=ALU.mult), writes=[g_sub.buf])
    cw = sb("cw", [128, 3, 2 * NCH])
    cb = sb("cb", [128, 2 * NCH])
    for j in range(3):
        fw.dma("sp", cw.ap[:, j, :], I["conv_w"][j:j + 1, :].rearrange("o (u p) -> p (o u)", p=128), writes=[cw.buf], allow_slow_non_contiguous=True)
    fw.dma("sp", cb.ap, I["conv_b"].rearrange("o (u p) -> p (o u)", p=128), writes=[cb.buf], allow_slow_non_contiguous=True)
    w_uq = sb("w_uq", [128, 2, 768], BF16)
    w_uk = sb("w_uk", [128, 2, 512], BF16)
    w_uv = sb("w_uv", [128, 2, 512], BF16)
    fw.dma("pool", w_uq.ap, I["w_uq"].rearrange("(kc p) n -> p kc n", p=128), writes=[w_uq.buf])
    fw.dma("pool", w_uk.ap, I["w_uk"].rearrange("(kc p) n -> p kc n", p=128), writes=[w_uk.buf])
    fw.dma("pool", w_uv.ap, I["w_uv"].rearrange("(kc p) n -> p kc n", p=128), writes=[w_uv.buf])
    bgate = sb("bgate", [1, D], BF16)
    fw.dma("pool", bgate.ap, I["b_gate"], writes=[bgate.buf])
    ones_r = sb("ones_r", [1, 128], BF16)
    fw.op("dve", lambda e: e.memset(ones_r.ap, 1.0), writes=[ones_r.buf])

    SLOT_E = 8 * 544
    NSLOT = 3
    slots = [sb(f"wslot{i}", [128, SLOT_E], BF16) for i in range(NSLOT)]
    groups = {}
    sched = []

    def conv_group(name, n, pieces):
        scr = dram("ws_" + name, [128, n])
        si = len(groups) % NSLOT
        st = slots[si]
        for (dst_fn, src_ap) in pieces:
            fw.dma("pool", dst_fn(st.ap), src_ap, writes=[st.buf])
        fw.dma("sp", scr.ap, st.ap[:, 0:n], reads=[st.buf], writes=[scr.buf], owner=st.buf)
        groups[name] = (scr, n)
        sched.append(name)

    def kview(ap, kc, n, c0=0, c1=None):
        v = ap[:, 0:kc * n].rearrange("p (k n) -> p k n", k=kc)
        return v if c1 is None else v[:, :, c0:c1]
    w_in_v = I["w_in"].rearrange("(kc p) n -> p kc n", p=128)
    conv_group("in0", 8 * 544, [(lambda a: kview(a, 8, 544), w_in_v[:, :, 0:544])])
    for nm, c0 in (("dq", 544), ("dk", 1056), ("dv", 1568)):
        conv_group(nm, 8 * 512, [(lambda a: kview(a, 8, 512), w_in_v[:, :, c0:c0 + 512])])
    w_o_v = I["w_o"].rearrange("(kc p) n -> p kc n", p=128)
    for i in range(2):
        conv_group(f"wo{i}", 8 * 512, [(lambda a: kview(a, 8, 512), w_o_v[:, :, i * 512:(i + 1) * 512])])
    w_up_v = I["w_up"].rearrange("(kc p) n -> p kc n", p=128)
    for c in range(NCH):
        conv_group(f"up{c}", 8 * 256, [
            (lambda a: kview(a, 8, 256, 0, 128), w_up_v[:, :, c * 128:(c + 1) * 128]),
            (lambda a: kview(a, 8, 256, 128, 256), w_up_v[:, :, D_FF + c * 128:D_FF + (c + 1) * 128])])
    DN = [(i * 4, min(4, NCH - i * 4)) for i in range((NCH + 3) // 4)]
    for i, (c0, ncc) in enumerate(DN):
        src = I["w_down"][c0 * 128:(c0 + ncc) * 128, :].rearrange("(ci p) n -> p ci n", p=128)
        conv_group(f"dn{i}", ncc * 1024, [(lambda a, ncc=ncc: kview(a, ncc, 1024), src)])
    w_g_v = I["w_gate"].rearrange("(kc p) n -> p kc n", p=128)
    for i in range(2):
        conv_group(f"gt{i}", 8 * 512, [(lambda a: kview(a, 8, 512), w_g_v[:, :, i * 512:(i + 1) * 512])])
    w_p_v = I["w_ple"].rearrange("(kc p) n -> p kc n", p=128)
    for i in range(2):
        conv_group(f"pl{i}", 2 * 512, [(lambda a: kview(a, 2, 512), w_p_v[:, :, i * 512:(i + 1) * 512])])

    FRONT_GROUPS = ["in0", "dk", "dv", "dq"]
    fslot = sb("wslot_front", [128, SLOT_E], BF16)

    class Feeder:
        def __init__(self, names, ring):
            self.names = names
            self.ring = ring
            self.i = 0
            self.loaded = 0

        def _load(self):
            nm = self.names[self.loaded % len(self.names)]
            scr, n = groups[nm]
            st = self.ring[self.loaded % len(self.ring)]
            fw.dma("sp", st.ap[:, 0:n], scr.ap, reads=[scr.buf], writes=[st.buf])
            self.loaded += 1

        def prefetch(self):
            if self.loaded < self.i + len(self.ring):
                self._load()

        def take(self, name):
            assert self.names[self.i % len(self.names)] == name, (self.names[self.i % len(self.names)], name)
            while self.loaded < self.i + len(self.ring):
                self._load()
            st = self.ring[self.i % len(self.ring)]
            self.i += 1
            return st
    feeder = Feeder([n_ for n_ in sched if n_ not in FRONT_GROUPS], slots)
    ffront = Feeder(FRONT_GROUPS, [fslot])

    XS = [sb("xs", [128, 2, D]) for _ in range(2)]
    T8 = sb("T8", [128, 8, T], BF16)
    XSB = [[Buf("xs00"), Buf("xs01")], [Buf("xs10"), Buf("xs11")]]
    T8b = [Buf("T80"), Buf("T81")]
    T8f = sb("T8f", [128, 8, T], BF16)
    T8fb = [Buf("T8f0"), Buf("T8f1")]
    cqT = sb("cqT", [128, 2, T], BF16)
    ckvT = sb("ckvT", [128, 2, T], BF16)
    pT = sb("pT", [128, 2, T], BF16)
    QAT = sb("QAT", [96, H_A, T], BF16)
    DQT = sb("DQT", [128, H_B, T], BF16)
    KVc = [{"KAT": sb("KATc", [96, H_A * T], BF16), "VA": sb("VAc", [128, 2 * H_A * 65], BF16),
            "DKT": sb("DKTc", [128, H_B * T], BF16), "VB": sb("VBc", [128, 2 * H_B * 129], BF16)} for _ in range(2)]
    KVpA = [{"KAT": sb("KATp", [96, 4 * T], BF16), "VA": sb("VAp", [128, 2 * 4 * 65], BF16)} for _ in range(2)]
    KVpB = [{"DKT": sb("DKTp", [128, T], BF16), "VB": sb("VBp", [128, 2 * 129], BF16)} for _ in range(2)]
    for kv in KVc:
        fw.op("pool", lambda e, a=kv["VA"].ap: e.memset(a, 1.0), writes=[kv["VA"].buf])
        fw.op("pool", lambda e, a=kv["VB"].ap: e.memset(a, 1.0), writes=[kv["VB"].buf])
    arena = sb("arena", [128, NCH * T], BF16)
    mix_f = arena.ap.bitcast(F32)[:, 0:2 * D].rearrange("p (b n) -> p b n", b=2)
    actT = arena.ap.rearrange("p (c t) -> p c t", c=NCH)
    actB = [Buf(f"actB{g_}") for g_ in range((NCH + 3) // 4)]
    junk = sb("junk", [128, D], BF16)
    PT = [sb(f"PT{i}", [128, T], BF16) for i in range(5)]
    ptc = [0]
    cq_f = [sb("cq_f", [128, D_CQ]) for _ in range(2)]
    ckv_f = [sb("ckv_f", [128, D_CKV]) for _ in range(2)]
    kr_f = [sb("kr_f", [128, D_ROPE]) for _ in range(2)]
    dq_f = [sb("dq_f", [128, W_B]) for _ in range(2)]
    dk_f = [sb("dk_f", [128, W_B]) for _ in range(2)]
    dv_f = [sb("dv_f", [128, W_B]) for _ in range(2)]
    q_f = [sb("q_f", [128, 768]) for _ in range(2)]
    kA_f = [sb("kA_f", [128, H_A * 96]) for _ in range(2)]
    p_f = [sb("p_f", [128, D_PLE]) for _ in range(2)]
    rtmp = sb("rtmp", [128, 4, 128])
    ob = [sb("ob", [128, 2, 128]) for _ in range(2)]
    junk2 = sb("junk2", [128, D], BF16)
    junk_f = sb("junk_f", [128, D])
    ybuf = [sb("ybuf", [128, 512]) for _ in range(2)]
    yc = [0]
    ub = [[sb("ub", [128, T + 2]) for _ in range(2)] for _ in range(2)]
    zb = [[sb("zb", [128, T]) for _ in range(2)] for _ in range(2)]
    halo = sb("halo", [128, 2 * NCH, 2])
    stats = [sb(f"st{i}", [128, 16]) for i in range(12)]
    stc = [0]

    def newstat():
        stc[0] += 1
        st = stats[stc[0] % len(stats)]
        fw.op("pool", lambda e, a=st.ap: e.memset(a, 0.0), writes=[st.buf])
        return st

    def kv_scratch(ngroups):
        return [{"KAT": dram("kvK", [96, H_A * T]), "VA": dram("kvVA", [128, 2 * H_A * 65]),
                 "DKT": dram("kvDK", [128, H_B * T]), "VB": dram("kvVB", [128, 2 * H_B * 129])} for _ in range(ngroups)]

    def tr_group(srcs, src_bufs, dst3, dst_bufs, nt, ncols):
        bk = G.get()
        n = len(srcs)
        for j, s_ap in enumerate(srcs):
            fw.op("pe", lambda e, o=bk.ap[0:ncols, j * 128:j * 128 + nt], i=s_ap: e.transpose(out=o, in_=i, identity=ident.ap[0:nt, 0:nt]),
                  reads=list(src_bufs) + [ident.buf], writes=[bk.buf])
        src3 = bk.ap[0:ncols, 0:n * 128].rearrange("p (j t) -> p j t", j=n)[:, :, 0:nt]
        copy(dst3, src3, reads=[bk.buf], writes=dst_bufs)

    def mm_tok(nt, lhs_fn, KC, rhs_fn, ncols, reads, bank=None, col0=0, extra=None):
        bk = bank if bank is not None else G.get()
        nmm = KC + (1 if extra is not None else 0)
        for kc in range(KC):
            fw.op("pe", lambda e, o=bk.ap[0:nt, col0:col0 + ncols], l=lhs_fn(kc), r=rhs_fn(kc), a=(kc == 0), z=(kc == nmm - 1):
                  e.matmul(out=o, lhsT=l, rhs=r, start=a, stop=z), reads=reads, writes=[bk.buf])
        if extra is not None:
            l, r, rb = extra
            fw.op("pe", lambda e, o=bk.ap[0:nt, col0:col0 + ncols], l=l, r=r: e.matmul(out=o, lhsT=l, rhs=r, start=False, stop=True),
                  reads=list(reads) + rb, writes=[bk.buf])
        return bk

    def rms_stat(nt, src_ap, n, src_bufs, st, col):
        fw.op("act", lambda e: e.activation(out=junk.ap[0:nt, 0:n], in_=src_ap, func=AF.Square, accum_out=st.ap[0:nt, col:col + 1]),
              reads=src_bufs, writes=[junk.buf, st.buf])
        fw.op("act", lambda e: e.activation(out=st.ap[0:nt, col + 1:col + 2], in_=st.ap[0:nt, col:col + 1], func=AF.Sqrt, bias=eps_t.ap[0:nt, :], scale=1.0 / n),
              reads=[eps_t.buf], writes=[st.buf])
        fw.op("dve", lambda e: e.reciprocal(out=st.ap[0:nt, col + 2:col + 3], in_=st.ap[0:nt, col + 1:col + 2]), writes=[st.buf])
        return st.ap[0:nt, col + 2:col + 3]

    def rope(nt, x3, bufs, blk, H):
        c = cos_t.ap[0:nt, blk, :].unsqueeze(1).to_broadcast([nt, H, 16])
        s = sin_t.ap[0:nt, blk, :].unsqueeze(1).to_broadcast([nt, H, 16])
        x1 = x3[:, :, 0:16]
        x2 = x3[:, :, 16:32]
        tv = [rtmp.ap[0:nt, i, 0:H * 16].rearrange("p (h d) -> p h d", h=H) for i in range(4)]
        rd = list(bufs) + [cos_t.buf, sin_t.buf]
        fw.op("dve", lambda e: e.tensor_tensor(out=tv[0], in0=x1, in1=c, op=ALU.mult), reads=rd, writes=[rtmp.buf])
        fw.op("dve", lambda e: e.tensor_tensor(out=tv[1], in0=x2, in1=s, op=ALU.mult), reads=rd, writes=[rtmp.buf])
        fw.op("dve", lambda e: e.tensor_tensor(out=tv[2], in0=x1, in1=s, op=ALU.mult), reads=rd, writes=[rtmp.buf])
        fw.op("dve", lambda e: e.tensor_tensor(out=tv[3], in0=x2, in1=c, op=ALU.mult), reads=rd, writes=[rtmp.buf])
        fw.op("dve", lambda e: e.tensor_tensor(out=x1, in0=tv[0], in1=tv[1], op=ALU.subtract), reads=[rtmp.buf], writes=bufs)
        fw.op("dve", lambda e: e.tensor_tensor(out=x2, in0=tv[2], in1=tv[3], op=ALU.add), reads=[rtmp.buf], writes=bufs)

    def layernorm(items, g, b):
        sts = [newstat() for _ in items]
        jk = [junk, junk2]

        def stage(fn):
            for i, (nt, x_ap, xb_) in enumerate(items):
                fn(i, nt, x_ap, sts[i], sts[i].ap, [xb_])
        stage(lambda i, nt, x, st, s, x_bufs: fw.op("act", lambda e: e.activation(out=jk[i % 2].ap[0:nt, :], in_=x, func=AF.Identity, accum_out=s[0:nt, 0:1]), reads=x_bufs, writes=[jk[i % 2].buf, st.buf]))
        stage(lambda i, nt, x, st, s, x_bufs: fw.op("act", lambda e: e.activation(out=jk[i % 2].ap[0:nt, :], in_=x, func=AF.Square, accum_out=s[0:nt, 1:2]), reads=x_bufs, writes=[jk[i % 2].buf, st.buf]))
        stage(lambda i, nt, x, st, s, x_bufs: fw.op("dve", lambda e: e.tensor_scalar(out=s[0:nt, 2:3], in0=s[0:nt, 0:1], scalar1=1.0 / D, scalar2=None, op0=ALU.mult), writes=[st.buf]))
        stage(lambda i, nt, x, st, s, x_bufs: fw.op("dve", lambda e: e.tensor_tensor(out=s[0:nt, 3:4], in0=s[0:nt, 2:3], in1=s[0:nt, 2:3], op=ALU.mult), writes=[st.buf]))
        stage(lambda i, nt, x, st, s, x_bufs: fw.op("dve", lambda e: e.scalar_tensor_tensor(out=s[0:nt, 4:5], in0=s[0:nt, 1:2], scalar=1.0 / D, in1=s[0:nt, 3:4], op0=ALU.mult, op1=ALU.subtract), writes=[st.buf]))
        stage(lambda i, nt, x, st, s, x_bufs: fw.op("act", lambda e: e.activation(out=s[0:nt, 5:6], in_=s[0:nt, 4:5], func=AF.Sqrt, bias=eps_t.ap[0:nt, :], scale=1.0), reads=[eps_t.buf], writes=[st.buf]))
        stage(lambda i, nt, x, st, s, x_bufs: fw.op("dve", lambda e: e.reciprocal(out=s[0:nt, 6:7], in_=s[0:nt, 5:6]), writes=[st.buf]))
        stage(lambda i, nt, x, st, s, x_bufs: fw.op("dve", lambda e: e.scalar_tensor_tensor(out=s[0:nt, 7:8], in0=s[0:nt, 2:3], scalar=-1.0, in1=s[0:nt, 6:7], op0=ALU.mult, op1=ALU.mult), writes=[st.buf]))
        stage(lambda i, nt, x, st, s, x_bufs: fw.op("act", lambda e: e.activation(out=x, in_=x, func=AF.Identity, bias=s[0:nt, 7:8], scale=s[0:nt, 6:7]), reads=[st.buf], writes=x_bufs))
        stage(lambda i, nt, x, st, s, x_bufs: fw.op("dve", lambda e: e.tensor_tensor(out=x, in0=x, in1=g.ap[0:nt, :], op=ALU.mult), reads=[g.buf], writes=x_bufs))
        stage(lambda i, nt, x, st, s, x_bufs: fw.op("dve", lambda e: e.tensor_tensor(out=x, in0=x, in1=b.ap[0:nt, :], op=ALU.add), reads=[b.buf], writes=x_bufs))

    def build_kv(kvc, blocks, srcs):
        KAT3 = kvc["KAT"].ap.rearrange("p (h t) -> p h t", h=H_A)
        VA4 = kvc["VA"].ap.rearrange("p (k h d) -> p k h d", k=2, h=H_A)
        DKT3 = kvc["DKT"].ap.rearrange("p (h t) -> p h t", h=H_B)
        VB4 = kvc["VB"].ap.rearrange("p (k h d) -> p k h d", k=2, h=H_B)
        B_ = list(enumerate(blocks))
        for b, (off, nt) in B_:
            ckv = srcs[b][0]
            tr_group([ckv.ap[0:nt, kc * 128:(kc + 1) * 128] for kc in range(2)], [ckv.buf], ckvT.ap[:, :, off:off + nt], [ckvT.buf], nt, 128)
            dv = srcs[b][3]
            copy(VB4[0:nt, b, :, 0:128], dv.ap[0:nt, :].rearrange("p (h d) -> p h d", h=H_B), reads=[dv.buf], writes=[kvc["VB"].buf], eng="pool")
        yield
        for b, (off, nt) in B_:
            dk = srcs[b][2]
            tr_group([dk.ap[0:nt, h * 128:(h + 1) * 128] for h in range(H_B)], [dk.buf], DKT3[:, :, off:off + nt], [kvc["DKT"].buf], nt, 128)
        yield
        for b, (off, nt) in B_:
            kA = kA_f[b]
            kr = srcs[b][1]
            kA3 = kA.ap.rearrange("p (h d) -> p h d", h=H_A)
            bk = mm_tok(nt, lambda kc: ckvT.ap[:, kc, off:off + nt], 2, lambda kc: w_uk.ap[:, kc, :], 512, [ckvT.buf, w_uk.buf])
            copy(kA3[0:nt, :, 0:64], bk.ap[0:nt, :].rearrange("p (h d) -> p h d", h=H_A), reads=[bk.buf], writes=[kA.buf])
            fw.op("pool", lambda e, o=kA3[0:nt, :, 64:96], i=kr.ap[0:nt, :].unsqueeze(1).to_broadcast([nt, H_A, 32]): e.tensor_copy(out=o, in_=i),
                  reads=[kr.buf], writes=[kA.buf])
        yield
        for b, (off, nt) in B_:
            bk = mm_tok(nt, lambda kc: ckvT.ap[:, kc, off:off + nt], 2, lambda kc: w_uv.ap[:, kc, :], 512, [ckvT.buf, w_uv.buf])
            copy(VA4[0:nt, b, :, 0:64], bk.ap[0:nt, :].rearrange("p (h d) -> p h d", h=H_A), reads=[bk.buf], writes=[kvc["VA"].buf])
        yield
        for hg in range(2):
            for b, (off, nt) in B_:
                kA3 = kA_f[b].ap.rearrange("p (h d) -> p h d", h=H_A)
                tr_group([kA3[0:nt, hg * 4 + j, :] for j in range(4)], [kA_f[b].buf], KAT3[:, hg * 4:hg * 4 + 4, off:off + nt], [kvc["KAT"].buf], nt, 96)
            yield

    def store_kv(kvc, scr):
        for k in ("KAT", "VA", "DKT", "VB"):
            fw.dma("sp", scr[k].ap, kvc[k].ap, reads=[kvc[k].buf], writes=[scr[k].buf], owner=kvc[k].buf)

    slot_hist = {"A": [None, None], "B": [None, None]}
    ldc = {"A": 0, "B": 0}

    def attn_tile(tile, kvc):
        ntok, blocks, pos0 = tile["ntok"], tile["blocks"], tile["pos0"]
        scr_groups = tile["past"]
        ngr = tile["npast"]
        passes = [("A", [0, 1, 2, 3]), ("A", [4, 5, 6, 7])] + [("B", [hb]) for hb in range(H_B)]
        units = []
        loads = []
        PS = []

        for pi, (kind, heads) in enumerate(passes):
            P = {"kind": kind, "heads": heads, "accs": None, "first": len(units), "last_idx": {}}
            PS.append(P)
            W = 65 if kind == "A" else 129

            def region(h, m, qb, P=P, kind=kind, heads=heads):
                if kind == "A":
                    i = heads.index(h)
                    return P["accs"][i // 3], (i % 3) * 130 + qb * 65
                return P["accs"][m], qb * 129
            P["region"] = region

            def make_loader(g, slot, kind=kind, heads=heads):
                def ld():
                    h0 = heads[0]
                    if kind == "A":
                        fw.dma("sp", slot["KAT"].ap, scr_groups[g]["KAT"].ap[:, h0 * T:(h0 + 4) * T], reads=[scr_groups[g]["KAT"].buf], writes=[slot["KAT"].buf])
                        fw.dma("sp", slot["VA"].ap.rearrange("p (k x) -> p k x", k=2),
                               scr_groups[g]["VA"].ap.rearrange("p (k x) -> p k x", k=2)[:, :, h0 * 65:(h0 + 4) * 65],
                               reads=[scr_groups[g]["VA"].buf], writes=[slot["VA"].buf])
                    else:
                        fw.dma("sp", slot["DKT"].ap, scr_groups[g]["DKT"].ap[:, h0 * T:(h0 + 1) * T], reads=[scr_groups[g]["DKT"].buf], writes=[slot["DKT"].buf])
                        fw.dma("sp", slot["VB"].ap.rearrange("p (k x) -> p k x", k=2),
                               scr_groups[g]["VB"].ap.rearrange("p (k x) -> p k x", k=2)[:, :, h0 * 129:(h0 + 1) * 129],
                               reads=[scr_groups[g]["VB"].buf], writes=[slot["VB"].buf])
                return ld

            def make_unit(g, cur, src, h, hl, ki, nk, n0, dg, m, kind=kind, region=region, W=W):
                st = {}

                def s1():
                    s0 = g * T + ki * 128
                    bk = G.get()
                    if kind == "A":
                        Ksrc = src["KAT"]
                        lhsT = Ksrc.ap.rearrange("p (h t) -> p h t", h=(H_A if cur else 4))[0:96, hl, ki * 128:ki * 128 + nk]
                        rhs = QAT.ap[0:96, h, n0:ntok]
                        qbuf = QAT.buf
                    else:
                        Ksrc = src["DKT"]
                        nh = H_B if cur else 1
                        lhsT = Ksrc.ap.rearrange("p (h t) -> p h t", h=nh)[m * 64:(m + 1) * 64, hl, ki * 128:ki * 128 + nk]
                        rhs = DQT.ap[m * 64:(m + 1) * 64, h, n0:ntok]
                        qbuf = DQT.buf
                    nq = ntok - n0
                    fw.op("pe", lambda e, o=bk.ap[0:nk, 0:nq], l=lhsT, r=rhs: e.matmul(out=o, lhsT=l, rhs=r, start=True, stop=True),
                          reads=[Ksrc.buf, qbuf], writes=[bk.buf])
                    pt = PT[ptc[0] % len(PT)]
                    ptc[0] += 1
                    if kind == "A":
                        fw.op("act", lambda e, o=pt.ap[0:nk, 0:nq], i=bk.ap[0:nk, 0:nq]: e.activation(out=o, in_=i, func=AF.Exp, scale=SC_A),
                              reads=[bk.buf], writes=[pt.buf])
                    else:
                        dd = (s0 - pos0) // 128 + BOFF
                        fw.op("act", lambda e, o=pt.ap[0:nk, 0:nq], i=bk.ap[0:nk, 0:nq], bb=bias_t.ap[0:nk, h, dd:dd + 1]:
                              e.activation(out=o, in_=i, func=AF.Exp, scale=SC_B, bias=bb), reads=[bk.buf, bias_t.buf], writes=[pt.buf])
                    if dg and not (kind == "A" and ntok < 128):
                        di = 0 if kind == "A" else 1 + h
                        wq = min(128, nq)
                        fw.op("pool", lambda e, o=pt.ap[0:nk, 0:wq], t=diag_t.ap[0:nk, di, 0:wq]: e.tensor_tensor(out=o, in0=o, in1=t, op=ALU.mult),
                              reads=[diag_t.buf], writes=[pt.buf])
                    st["pt"] = pt

                def s2():
                    pt = st["pt"]
                    if kind == "A":
                        Vsrc = src["VA"]
                        vv = Vsrc.ap.rearrange("p (k h d) -> p k h d", k=2, h=(H_A if cur else 4))[0:nk, ki, hl, :]
                    else:
                        Vsrc = src["VB"]
                        vv = Vsrc.ap.rearrange("p (k h d) -> p k h d", k=2, h=(H_B if cur else 1))[0:nk, ki, hl, :]
                    for qb, (qoff, qnt) in enumerate(blocks):
                        if qoff < n0:
                            continue
                        acc, c0 = region(h, m, qb)
                        fw.op("pe", lambda e, o=acc.ap[0:qnt, c0:c0 + W], l=pt.ap[0:nk, qoff - n0:qoff - n0 + qnt], r=vv:
                              e.matmul(out=o, lhsT=l, rhs=r, start=False, stop=False, skip_group_check=True),
                              reads=[pt.buf, Vsrc.buf], writes=[acc.buf])
                return s1, s2

            for g in range(ngr + 1):
                cur = (g == ngr)
                L = None
                if cur:
                    src = kvc
                    ktiles = [(i, nt, off, True) for i, (off, nt) in enumerate(blocks)]
                else:
                    si = ldc[kind] % 2
                    ldc[kind] += 1
                    src = (KVpA if kind == "A" else KVpB)[si]
                    L = {"kind": kind, "slot": si, "first": len(units), "last": None, "fn": make_loader(g, src), "done": False}
                    loads.append(L)
                    ktiles = [(0, 128, 0, False), (1, 128, 0, False)]
                for h in heads:
                    hl = h if cur else h - heads[0]
                    for (ki, nk, n0, dg) in ktiles:
                        for m in range(1 if kind == "A" else 2):
                            s1, s2 = make_unit(g, cur, src, h, hl, ki, nk, n0, dg, m)
                            units.append({"P": P, "s1": s1, "s2": s2})
                            P["last_idx"][h] = len(units) - 1
                if L is not None:
                    L["last"] = len(units) - 1
            P["end"] = len(units) - 1

        prev_last = {("A", 0): -1, ("A", 1): -1, ("B", 0): -1, ("B", 1): -1}
        for L in loads:
            L["prev_last"] = prev_last[(L["kind"], L["slot"])]
            prev_last[(L["kind"], L["slot"])] = L["last"]

        nqb = len(blocks)
        qn = blocks[0][1]

        def evac_head(P, h):
            kind, region = P["kind"], P["region"]
            st = newstat()
            if kind == "A":
                acc, c0 = region(h, 0, 0)
                av = acc.ap[0:qn, c0:c0 + nqb * 65].rearrange("p (q d) -> p q d", q=nqb)
                fw.op("dve", lambda e, o=st.ap[0:qn, 0:nqb].unsqueeze(2), i=av[:, :, 64:65]: e.reciprocal(out=o, in_=i), reads=[acc.buf], writes=[st.buf])
                fw.op("dve", lambda e, o=mix_f[0:qn, 0:nqb, h * 64:(h + 1) * 64], i=av[:, :, 0:64], r=st.ap[0:qn, 0:nqb].unsqueeze(2).to_broadcast([qn, nqb, 64]):
                      e.tensor_tensor(out=o, in0=i, in1=r, op=ALU.mult), reads=[acc.buf, st.buf], writes=[arena.buf] + actB)
            else:
                a0, _ = region(h, 0, 0)
                a1, _ = region(h, 1, 0)
                v0 = a0.ap[0:qn, 0:nqb * 129].rearrange("p (q d) -> p q d", q=nqb)
                v1 = a1.ap[0:qn, 0:nqb * 129].rearrange("p (q d) -> p q d", q=nqb)
                s = st.ap
                tmp = ob[h % 2]
                dst = mix_f[0:qn, 0:nqb, 512 + h * 128:512 + (h + 1) * 128]
                fw.op("dve", lambda e, o=s[0:qn, 0:nqb].unsqueeze(2), i=v0[:, :, 128:129]: e.reciprocal(out=o, in_=i), reads=[a0.buf], writes=[st.buf])
                fw.op("dve", lambda e, o=s[0:qn, 2:2 + nqb].unsqueeze(2), i=v1[:, :, 128:129]: e.reciprocal(out=o, in_=i), reads=[a1.buf], writes=[st.buf])
                fw.op("dve", lambda e, o=s[0:qn, 4:4 + nqb], i=s[0:qn, 2:2 + nqb], l=nlam[0:qn, :]: e.tensor_scalar(out=o, in0=i, scalar1=l, scalar2=None, op0=ALU.mult),
                      reads=[lam_s.buf], writes=[st.buf])
                fw.op("dve", lambda e, o=dst, i=v0[:, :, 0:128], r=s[0:qn, 0:nqb].unsqueeze(2).to_broadcast([qn, nqb, 128]):
                      e.tensor_tensor(out=o, in0=i, in1=r, op=ALU.mult), reads=[a0.buf, st.buf], writes=[arena.buf] + actB)
                fw.op("dve", lambda e, o=tmp.ap[0:qn, 0:nqb, :], i=v1[:, :, 0:128], r=s[0:qn, 4:4 + nqb].unsqueeze(2).to_broadcast([qn, nqb, 128]):
                      e.tensor_tensor(out=o, in0=i, in1=r, op=ALU.mult), reads=[a1.buf, st.buf], writes=[tmp.buf])
                fw.op("dve", lambda e, o=dst, t=tmp.ap[0:qn, 0:nqb, :]: e.tensor_tensor(out=o, in0=o, in1=t, op=ALU.add), reads=[tmp.buf], writes=[arena.buf] + actB)

        def subln_all():
            st = newstat()
            for qb in range(nqb):
                x4 = mix_f[0:qn, qb, 512:1024].rearrange("p (h d) -> p h d", h=H_B)
                sq = junk_f.ap[0:qn, qb * 512:(qb + 1) * 512].rearrange("p (h d) -> p h d", h=H_B)
                fw.op("dve", lambda e, o=sq, i=x4: e.tensor_tensor(out=o, in0=i, in1=i, op=ALU.mult), reads=[arena.buf], writes=[junk_f.buf])
                fw.op("dve", lambda e, o=st.ap[0:qn, qb * 4:qb * 4 + 4], i=sq: e.reduce_sum(out=o, in_=i, axis=mybir.AxisListType.X), reads=[junk_f.buf], writes=[st.buf])
            n4 = nqb * 4
            fw.op("act", lambda e: e.activation(out=st.ap[0:qn, 8:8 + n4], in_=st.ap[0:qn, 0:n4], func=AF.Sqrt, bias=eps_t.ap[0:qn, :], scale=1.0 / 128),
                  reads=[eps_t.buf], writes=[st.buf])
            fw.op("dve", lambda e: e.reciprocal(out=st.ap[0:qn, 0:n4], in_=st.ap[0:qn, 8:8 + n4]), writes=[st.buf])
            for qb in range(nqb):
                x4 = mix_f[0:qn, qb, 512:1024].rearrange("p (h d) -> p h d", h=H_B)
                fw.op("dve", lambda e, o=x4, r=st.ap[0:qn, qb * 4:qb * 4 + 4].unsqueeze(2).to_broadcast([qn, H_B, 128]):
                      e.tensor_tensor(out=o, in0=o, in1=r, op=ALU.mult), reads=[st.buf], writes=[arena.buf] + actB)
                fw.op("dve", lambda e, o=x4, gq=g_sub.ap[0:qn, :].unsqueeze(1).to_broadcast([qn, H_B, 128]):
                      e.tensor_tensor(out=o, in0=o, in1=gq, op=ALU.mult), reads=[g_sub.buf], writes=[arena.buf] + actB)

        def pass_begin(P):
            P["accs"] = [V.get() for _ in range(2)]
            for a in P["accs"]:
                fw.op("dve", lambda e, ap=a.ap: e.memset(ap, 0.0), writes=[a.buf])

        DEPTH = 3
        nU = len(units)

        def pump_loads(done_upto, need_first=None):
            for L in loads:
                if L["done"]:
                    continue
                if L["prev_last"] <= done_upto:
                    L["fn"]()
                    L["done"] = True
                else:
                    assert need_first is None or L["first"] > need_first, "load not emittable before first use"
        def run():
          for i in range(nU + DEPTH):
              if i < nU:
                  pump_loads(i - DEPTH - 1 if i - DEPTH - 1 >= -1 else -1, need_first=i)
                  units[i]["s1"]()
              j = i - DEPTH
              if j >= 0:
                  P = units[j]["P"]
                  if j == P["first"]:
                      if P["accs"] is None:
                          pass_begin(P)
                      pi_ = PS.index(P)
                      if pi_ + 1 < len(PS) and PS[pi_ + 1]["accs"] is None:
                          pass_begin(PS[pi_ + 1])
                  units[j]["s2"]()
                  pump_loads(j)
                  for h in P["heads"]:
                      if P["last_idx"][h] == j:
                          if P["kind"] == "B":
                              evac_head(P, h)
                          else:
                              hi = P["heads"].index(h)
                              if hi % 3 == 2 or hi == len(P["heads"]) - 1:
                                  for h2 in P["heads"][(hi // 3) * 3:hi + 1]:
                                      evac_head(P, h2)
          for _ in range(3):
              tile["pull"]()
          subln_all()
        return (lambda: pump_loads(-1)), run

    def front(tile):
        ntok, blocks, pos0 = tile["ntok"], tile["blocks"], tile["pos0"]
        outs = tile["out"]
        kvc = KVc[tile["idx"] % 2]
        nb = len(blocks)
        xs = XS[tile["par"]]
        xsb = XSB[tile["par"]]
        B_ = list(enumerate(blocks))
        if not tile.get("x_loaded"):
            for b, (off, nt) in B_:
                fw.dma("sp", xs.ap[0:nt, b, :], tile["x"][off:off + nt, :], writes=[xsb[b]])
            tile["x_loaded"] = True
        ffront.prefetch()
        yield
        for b, (off, nt) in B_:
            for g4 in range(2):
                tr_group([xs.ap[0:nt, b, (g4 * 4 + j) * 128:(g4 * 4 + j + 1) * 128] for j in range(4)], [xsb[b]],
                         T8f.ap[:, g4 * 4:g4 * 4 + 4, off:off + nt], [T8fb[b]], nt, 128)
            yield
        st_w = ffront.take("in0")
        wv = st_w.ap.rearrange("p (k n) -> p k n", k=8)
        sts = {}
        rr = {}
        pbk = {}
        for b, (off, nt) in B_:
            bk = mm_tok(nt, lambda kc: T8f.ap[:, kc, off:off + nt], 8, lambda kc: wv[:, kc, 0:512], 512, [T8fb[b], st_w.buf])
            copy(dq_f[b].ap[0:nt, :], bk.ap[0:nt, :], reads=[bk.buf], writes=[dq_f[b].buf])
            bk2 = mm_tok(nt, lambda kc: T8f.ap[:, kc, off:off + nt], 8, lambda kc: wv[:, kc, 512:544], 32, [T8fb[b], st_w.buf])
            copy(kr_f[b].ap[0:nt, :], bk2.ap[0:nt, 0:32], reads=[bk2.buf], writes=[kr_f[b].buf])
            yield
        ffront.prefetch()
        for b, (off, nt) in B_:
            sts[b] = newstat()
            rr[b] = (rms_stat(nt, dq_f[b].ap[0:nt, 0:256], 256, [dq_f[b].buf], sts[b], 0),
                     rms_stat(nt, dq_f[b].ap[0:nt, 256:512], 256, [dq_f[b].buf], sts[b], 3))
        yield
        for b, (off, nt) in B_:
            blk = (pos0 + off) // 128
            rope(nt, kr_f[b].ap[0:nt, :].unsqueeze(1), [kr_f[b].buf], blk, 1)
            fw.dma("sp", outs["kr"][off:off + nt, :], kr_f[b].ap[0:nt, :], reads=[kr_f[b].buf], owner=kr_f[b], final=True)
        yield
        for b, (off, nt) in B_:
            r1, r2 = rr[b]
            fw.op("dve", lambda e, o=cq_f[b].ap[0:nt, :], i=dq_f[b].ap[0:nt, 0:256], r=r1, gg=g_cq.ap[0:nt, :]: e.scalar_tensor_tensor(out=o, in0=i, scalar=r, in1=gg, op0=ALU.mult, op1=ALU.mult),
                  reads=[dq_f[b].buf, sts[b].buf, g_cq.buf], writes=[cq_f[b].buf])
            fw.op("dve", lambda e, o=ckv_f[b].ap[0:nt, :], i=dq_f[b].ap[0:nt, 256:512], r=r2, gg=g_ckv.ap[0:nt, :]: e.scalar_tensor_tensor(out=o, in0=i, scalar=r, in1=gg, op0=ALU.mult, op1=ALU.mult),
                  reads=[dq_f[b].buf, sts[b].buf, g_ckv.buf], writes=[ckv_f[b].buf])
            fw.dma("sp", outs["ckv"][off:off + nt, :], ckv_f[b].ap[0:nt, :], reads=[ckv_f[b].buf], owner=ckv_f[b], final=True)
        yield
        for nm, dst in (("dk", dk_f), ("dv", dv_f), ("dq", dq_f)):
            st_w = ffront.take(nm)
            wv = st_w.ap[:, 0:8 * 512].rearrange("p (k n) -> p k n", k=8)
            for b, (off, nt) in B_:
                bk = mm_tok(nt, lambda kc: T8f.ap[:, kc, off:off + nt], 8, lambda kc: wv[:, kc, 0:512], 512, [T8fb[b], st_w.buf])
                copy(dst[b].ap[0:nt, :], bk.ap[0:nt, :], reads=[bk.buf], writes=[dst[b].buf])
                if nm != "dq":
                    fw.dma("sp", outs[nm][off:off + nt, :], dst[b].ap[0:nt, :], reads=[dst[b].buf], owner=dst[b], final=True)
                yield
            ffront.prefetch()
            if nm == "dk":
                for b, (off, nt) in B_:
                    tr_group([cq_f[b].ap[0:nt, kc * 128:(kc + 1) * 128] for kc in range(2)], [cq_f[b].buf], cqT.ap[:, :, off:off + nt], [cqT.buf], nt, 128)
                yield
        for b, (off, nt) in B_:
            tr_group([dq_f[b].ap[0:nt, h * 128:(h + 1) * 128] for h in range(H_B)], [dq_f[b].buf], DQT.ap[:, :, off:off + nt], [DQT.buf], nt, 128)
        yield
        for b, (off, nt) in B_:
            for (c0, ncol) in ((0, 512), (512, 256)):
                bk = mm_tok(nt, lambda kc: cqT.ap[:, kc, off:off + nt], 2, lambda kc: w_uq.ap[:, kc, c0:c0 + ncol], ncol, [cqT.buf, w_uq.buf])
                copy(q_f[b].ap[0:nt, c0:c0 + ncol], bk.ap[0:nt, 0:ncol], reads=[bk.buf], writes=[q_f[b].buf])
            yield
        for b, (off, nt) in B_:
            blk = (pos0 + off) // 128
            q3 = q_f[b].ap.rearrange("p (h d) -> p h d", h=H_A)
            rope(nt, q3[0:nt, :, 64:96], [q_f[b].buf], blk, H_A)
        yield
        for _ in build_kv(kvc, blocks, [(ckv_f[b], kr_f[b], dk_f[b], dv_f[b]) for b in range(nb)]):
            yield
        if tile["store"] is not None:
            store_kv(kvc, tile["store"])
        for b, (off, nt) in B_:
            q3 = q_f[b].ap.rearrange("p (h d) -> p h d", h=H_A)
            for hg in range(2):
                tr_group([q3[0:nt, hg * 4 + j, :] for j in range(4)], [q_f[b].buf], QAT.ap[:, hg * 4:hg * 4 + 4, off:off + nt], [QAT.buf], nt, 96)
            yield
        tile["front_done"] = True

    def prepass(item):
        s_, g, scr, kidx = item
        kvc = KVc[kidx % 2]
        blocks = [(0, 128), (128, 128)]
        for b, (off, nt) in enumerate(blocks):
            r0 = g * T + off
            fw.dma("sp", ckv_f[b].ap, I["c_ckv"][s_, r0:r0 + 128, :], writes=[ckv_f[b].buf])
            fw.dma("sp", kr_f[b].ap, I["c_kr"][s_, r0:r0 + 128, :], writes=[kr_f[b].buf])
            fw.dma("sp", dk_f[b].ap, I["c_dk"][s_, r0:r0 + 128, :], writes=[dk_f[b].buf])
            fw.dma("sp", dv_f[b].ap, I["c_dv"][s_, r0:r0 + 128, :], writes=[dv_f[b].buf])
        yield
        for _ in build_kv(kvc, blocks, [(ckv_f[b], kr_f[b], dk_f[b], dv_f[b]) for b in range(2)]):
            yield
        store_kv(kvc, scr[g])
        yield

    def back(tile, pull):
        ntok, blocks, pos0 = tile["ntok"], tile["blocks"], tile["pos0"]
        psrc, outs = tile["p"], tile["out"]
        kvc = KVc[tile["idx"] % 2]
        nb = len(blocks)
        xs = XS[tile["par"]]
        xsb = XSB[tile["par"]]
        nxt = tile.get("next")
        if nxt is not None and not nxt.get("x_loaded"):
            for b, (off, nt) in enumerate(nxt["blocks"]):
                fw.dma("sp", XS[nxt["par"]].ap[0:nt, b, :], nxt["x"][off:off + nt, :], writes=[XSB[nxt["par"]][b]])
            nxt["x_loaded"] = True
        for b, (off, nt) in enumerate(blocks):
            fw.dma("sp", p_f[b].ap[0:nt, :], psrc[off:off + nt, :], writes=[p_f[b].buf])
        tile["pull"] = pull
        if "attn" not in tile:
            tile["attn"] = attn_tile(tile, kvc)
            tile["attn"][0]()
        tile["attn"][1]()
        for b, (off, nt) in enumerate(blocks):
            for g4 in range(2):
                tr_group([mix_f[0:nt, b, (g4 * 4 + j) * 128:(g4 * 4 + j + 1) * 128] for j in range(4)], [arena.buf],
                         T8.ap[:, g4 * 4:g4 * 4 + 4, off:off + nt], [T8b[b]], nt, 128)
        for i in range(2):
            st_w = feeder.take(f"wo{i}")
            wv = st_w.ap[:, 0:8 * 512].rearrange("p (k n) -> p k n", k=8)
            for b, (off, nt) in enumerate(blocks):
                bk = mm_tok(nt, lambda kc: T8.ap[:, kc, off:off + nt], 8, lambda kc: wv[:, kc, 0:512], 512, [T8b[b], st_w.buf])
                xa = xs.ap[0:nt, b, i * 512:(i + 1) * 512]
                fw.op("dve", lambda e, o=xa, p=bk.ap[0:nt, :]: e.scalar_tensor_tensor(out=o, in0=o, scalar=ALPHA, in1=p, op0=ALU.mult, op1=ALU.add),
                      reads=[bk.buf], writes=[xsb[b]])
        for _ in range(4):
            pull()
        layernorm([(nt, xs.ap[0:nt, b, :], xsb[b]) for b, (off, nt) in enumerate(blocks)], ln1g, ln1b)
        for b, (off, nt) in enumerate(blocks):
            for g4 in range(2):
                tr_group([xs.ap[0:nt, b, (g4 * 4 + j) * 128:(g4 * 4 + j + 1) * 128] for j in range(4)], [xsb[b]],
                         T8.ap[:, g4 * 4:g4 * 4 + 4, off:off + nt], [T8b[b]], nt, 128)
        W2 = ntok + 2
        if tile.get("halo0"):
            fw.op("pool", lambda e: e.memset(halo.ap, 0.0), writes=[halo.buf])
        if tile.get("conv_in") is not None:
            ci = tile["conv_in"]
            for pc in range(6):
                w_ = min(1024, 2 * D_FF - pc * 1024)
                fw.dma("sp", junk_f.ap[0:2, 0:w_], ci[:, pc * 1024:pc * 1024 + w_], writes=[junk_f.buf])
                bkh = G.get()
                nu = w_ // 128
                for ul in range(nu):
                    fw.op("pe", lambda e, o=bkh.ap[:, ul * 2:ul * 2 + 2], i=junk_f.ap[0:2, ul * 128:(ul + 1) * 128]: e.transpose(out=o, in_=i, identity=ident.ap[0:2, 0:2]),
                          reads=[junk_f.buf, ident.buf], writes=[bkh.buf])
                copy(halo.ap[:, pc * 8:pc * 8 + nu, :], bkh.ap[:, 0:nu * 2].rearrange("p (u t) -> p u t", t=2), reads=[bkh.buf], writes=[halo.buf])
        for c in range(NCH):
            st_w = feeder.take(f"up{c}")
            wv = st_w.ap[:, 0:8 * 256].rearrange("p (k n) -> p k n", k=8)
            bk = G.get()
            for hf in range(2):
                for kc in range(8):
                    fw.op("pe", lambda e, o=bk.ap[:, hf * 256:hf * 256 + ntok], l=wv[:, kc, hf * 128:(hf + 1) * 128], r=T8.ap[:, kc, 0:ntok], a=(kc == 0), z=(kc == 7):
                          e.matmul(out=o, lhsT=l, rhs=r, start=a, stop=z), reads=[st_w.buf] + T8b[0:nb], writes=[bk.buf])
            for hf in range(2):
                ui = c + hf * NCH
                u = ub[c % 2][hf]
                z = zb[c % 2][hf]
                ucol = u.ap[:, 0:W2]
                zc = z.ap[:, 0:ntok]
                fw.op("pool", lambda e, o=ucol[:, 0:2], i=halo.ap[:, ui, :]: e.tensor_copy(out=o, in_=i), reads=[halo.buf], writes=[u.buf])
                fw.op("act", lambda e, o=ucol[:, 2:W2], i=bk.ap[:, hf * 256:hf * 256 + ntok]: e.copy(out=o, in_=i), reads=[bk.buf], writes=[u.buf])
                fw.op("pool", lambda e, o=halo.ap[:, ui, :], i=ucol[:, ntok:W2]: e.tensor_copy(out=o, in_=i), reads=[u.buf], writes=[halo.buf])
                wj = [cw.ap[:, j, ui:ui + 1] for j in range(3)]
                fw.op("act", lambda e, o=zc, i=bk.ap[:, hf * 256:hf * 256 + ntok], w=wj[2], bb=cb.ap[:, ui:ui + 1]:
                      e.activation(out=o, in_=i, func=AF.Identity, bias=bb, scale=w), reads=[bk.buf, cw.buf, cb.buf], writes=[z.buf])
                fw.op("dve", lambda e, o=zc, i=ucol[:, 1:W2 - 1], w=wj[1]:
                      e.scalar_tensor_tensor(out=o, in0=i, scalar=w, in1=o, op0=ALU.mult, op1=ALU.add), reads=[u.buf, cw.buf], writes=[z.buf])
                fw.op("dve", lambda e, o=zc, i=ucol[:, 0:W2 - 2], w=wj[0]:
                      e.scalar_tensor_tensor(out=o, in0=i, scalar=w, in1=o, op0=ALU.mult, op1=ALU.add), reads=[u.buf, cw.buf], writes=[z.buf])
            zg_t, zv_t = zb[c % 2]
            zg = zg_t.ap[:, 0:ntok]
            fw.op("act", lambda e, o=zg: e.activation(out=o, in_=o, func=AF.Gelu), writes=[zg_t.buf])
            fw.op("dve", lambda e, o=actT[:, c, 0:ntok], a=zg, b2=zv_t.ap[:, 0:ntok]: e.tensor_tensor(out=o, in0=a, in1=b2, op=ALU.mult),
                  reads=[zg_t.buf, zv_t.buf], writes=[arena.buf, actB[c // 4]])
            pull()
            if c % 2 == 1:
                pull()
        if tile["conv_out"] is not None:
            co = tile["conv_out"]
            for pc in range(6):
                w_ = min(1024, 2 * D_FF - pc * 1024)
                nu = w_ // 128
                for hb2 in range((nu + 3) // 4):
                    bkh = G.get()
                    n4 = min(4, nu - hb2 * 4)
                    for ul in range(n4):
                        u_ = pc * 8 + hb2 * 4 + ul
                        fw.op("pe", lambda e, o=bkh.ap[0:2, ul * 128:(ul + 1) * 128], i=halo.ap[:, u_, :]: e.transpose(out=o, in_=i, identity=ident.ap),
                              reads=[halo.buf, ident.buf], writes=[bkh.buf])
                    copy(junk_f.ap[0:2, hb2 * 512:hb2 * 512 + n4 * 128], bkh.ap[0:2, 0:n4 * 128], reads=[bkh.buf], writes=[junk_f.buf])
                fw.dma("sp", co[:, pc * 1024:pc * 1024 + w_], junk_f.ap[0:2, 0:w_], reads=[junk_f.buf], owner=junk_f, final=True)
        accs = [[V.get() for _ in range(2)] for _ in range(nb)]
        for gi, (c0, ncc) in enumerate(DN):
            pull()
            st_w = feeder.take(f"dn{gi}")
            wv = st_w.ap[:, 0:ncc * 1024].rearrange("p (c n) -> p c n", c=ncc)
            for b, (off, nt) in enumerate(blocks):
                for hf in range(2):
                    for ci in range(ncc):
                        c = c0 + ci
                        fw.op("pe", lambda e, o=accs[b][hf].ap[0:nt, :], l=actT[:, c, off:off + nt], r=wv[:, ci, hf * 512:(hf + 1) * 512], a=(c == 0), z=(c == NCH - 1):
                              e.matmul(out=o, lhsT=l, rhs=r, start=a, stop=z), reads=[actB[gi], st_w.buf], writes=[accs[b][hf].buf])
        for b, (off, nt) in enumerate(blocks):
            for hf in range(2):
                xa = xs.ap[0:nt, b, hf * 512:(hf + 1) * 512]
                fw.op("dve", lambda e, o=xa, p=accs[b][hf].ap[0:nt, :]: e.scalar_tensor_tensor(out=o, in0=o, scalar=ALPHA, in1=p, op0=ALU.mult, op1=ALU.add),
                      reads=[accs[b][hf].buf], writes=[xsb[b]])
        for _ in range(4):
            pull()
        layernorm([(nt, xs.ap[0:nt, b, :], xsb[b]) for b, (off, nt) in enumerate(blocks)], ln2g, ln2b)
        if nxt is not None and nxt.get("conv_in") is None and nxt.get("front_done"):
            nxt["attn"] = attn_tile(nxt, KVc[nxt["idx"] % 2])
            nxt["attn"][0]()
        for b, (off, nt) in enumerate(blocks):
            for g4 in range(2):
                tr_group([xs.ap[0:nt, b, (g4 * 4 + j) * 128:(g4 * 4 + j + 1) * 128] for j in range(4)], [xsb[b]],
                         T8.ap[:, g4 * 4:g4 * 4 + 4, off:off + nt], [T8b[b]], nt, 128)
            tr_group([p_f[b].ap[0:nt, kc * 128:(kc + 1) * 128] for kc in range(2)], [p_f[b].buf], pT.ap[:, :, off:off + nt], [pT.buf], nt, 128)
        gts = {}
        for i in range(2):
            st_w = feeder.take(f"gt{i}")
            wv = st_w.ap[:, 0:8 * 512].rearrange("p (k n) -> p k n", k=8)
            for b, (off, nt) in enumerate(blocks):
                bk = mm_tok(nt, lambda kc: T8.ap[:, kc, off:off + nt], 8, lambda kc: wv[:, kc, 0:512], 512, [T8b[b], st_w.buf],
                            extra=(ones_r.ap[0:1, 0:nt], bgate.ap[0:1, i * 512:(i + 1) * 512], [ones_r.buf, bgate.buf]))
                gt = gate_all[(i * 2 + b) % 4]
                fw.op("act", lambda e, o=gt.ap[0:nt, :], p=bk.ap[0:nt, :]: e.activation(out=o, in_=p, func=AF.Sigmoid), reads=[bk.buf], writes=[gt.buf])
                gts[(i, b)] = gt
        for i in range(2):
            st_w = feeder.take(f"pl{i}")
            wv = st_w.ap[:, 0:1024].rearrange("p (k n) -> p k n", k=2)
            for b, (off, nt) in enumerate(blocks):
                bk = mm_tok(nt, lambda kc: pT.ap[:, kc, off:off + nt], 2, lambda kc: wv[:, kc, 0:512], 512, [pT.buf, st_w.buf])
                gt = gts[(i, b)]
                yb = ybuf[yc[0] % len(ybuf)]
                yc[0] += 1
                fw.op("dve", lambda e, o=gt.ap[0:nt, :], p=bk.ap[0:nt, :]: e.tensor_tensor(out=o, in0=o, in1=p, op=ALU.mult), reads=[bk.buf], writes=[gt.buf])
                fw.op("dve", lambda e, o=yb.ap[0:nt, :], a=gt.ap[0:nt, :], h2=xs.ap[0:nt, b, i * 512:(i + 1) * 512]: e.tensor_tensor(out=o, in0=a, in1=h2, op=ALU.add),
                      reads=[gt.buf, xsb[b]], writes=[yb.buf])
                fw.dma("sp", outs["y"][off:off + nt, i * 512:(i + 1) * 512], yb.ap[0:nt, :], reads=[yb.buf], owner=yb, final=True)

    gate_all = [sb("gate_all", [128, 512]) for _ in range(4)]

    work = []
    for s_ in range(NP):
        scr = kv_scratch(S // T)
        for j in range(S // T):
            tile = {"ntok": T, "blocks": [(0, 128), (128, 128)], "pos0": j * T,
                    "x": I["x_p"][s_, j * T:(j + 1) * T, :], "p": I["p_p"][s_, j * T:(j + 1) * T, :],
                    "out": {"y": O["y_p"][s_, j * T:(j + 1) * T, :], "ckv": O["ckv_p"][s_, j * T:(j + 1) * T, :],
                            "kr": O["kr_p"][s_, j * T:(j + 1) * T, :], "dk": O["dk_p"][s_, j * T:(j + 1) * T, :],
                            "dv": O["dv_p"][s_, j * T:(j + 1) * T, :]},
                    "npast": j, "past": scr, "store": scr[j] if j < S // T - 1 else None,
                    "conv_out": O["conv_p"][s_] if j == S // T - 1 else None, "halo0": (j == 0)}
            work.append(("tile", tile))
    for s_ in range(NS):
        ng = PAST // T
        scr = kv_scratch(ng)
        for g in range(ng):
            work.append(("pre", (s_, g, scr)))
        tile = {"ntok": TS, "blocks": [(0, TS)], "pos0": PAST,
                "x": I["x_s"][s_], "p": I["p_s"][s_],
                "out": {"y": O["y_s"][s_], "ckv": O["ckv_s"][s_], "kr": O["kr_s"][s_], "dk": O["dk_s"][s_], "dv": O["dv_s"][s_]},
                "npast": ng, "past": scr, "store": None, "conv_out": O["conv_s"][s_], "conv_in": I["c_conv"][s_]}
        work.append(("tile", tile))
    tiles = [w[1] for w in work if w[0] == "tile"]
    for i, t_ in enumerate(tiles):
        t_["par"] = i % 2
        t_["next"] = tiles[i + 1] if i + 1 < len(tiles) else None
    n_prompt_tiles = NP * (S // T)
    bg_items = [it for k_, it in work if k_ == "pre"][:max(0, n_prompt_tiles - 1)]
    bg_set = set(id(it) for it in bg_items)
    work = [(k_, it) for k_, it in work if not (k_ == "pre" and id(it) in bg_set)]
    bg_state = {"i": 0}
    for wi, (kind_, item) in enumerate(work):
        if kind_ == "tile":
            item["idx"] = wi
        else:
            work[wi] = (kind_, item + (wi,))

    def exhaust(gen):
        for _ in gen:
            pass

    def gen_for(kind_, item):
        return front(item) if kind_ == "tile" else prepass(item)

    import itertools
    wi = 0
    while True:
        kind_, item = work[wi]
        exhaust(gen_for(kind_, item))
        if kind_ == "tile":
            break
        wi += 1
    while wi < len(work):
        tile = work[wi][1]
        nxt_items = []
        wj = wi + 1
        while wj < len(work):
            nxt_items.append(work[wj])
            if work[wj][0] == "tile":
                break
            wj += 1
        gens = [gen_for(k_, it_) for k_, it_ in nxt_items]
        if tile.get("conv_in") is None and bg_state["i"] < len(bg_items) and nxt_items and nxt_items[-1][0] == "tile":
            gens.append(prepass(bg_items[bg_state["i"]] + (tile["idx"],)))
            bg_state["i"] += 1
        pending = itertools.chain.from_iterable(gens)
        state = {"live": True}

        def pull(pending=pending, state=state):
            if state["live"]:
                try:
                    next(pending)
                except StopIteration:
                    state["live"] = False
        back(tile, pull)
        while state["live"]:
            pull()
        wi = wj

    fw.finish()
    return nc, tables


_CACHE = {}


def run(inputs, NP, S, NS, PAST, ncores):
    key = (NP, S, NS, PAST)
    if key not in _CACHE:
        _CACHE[key] = build(NP, S, NS, PAST)
    nc, tables = _CACHE[key]
    f = lambda a: np.ascontiguousarray(np.asarray(a, dtype=np.float32))
    in_maps = []
    for c in range(ncores):
        ps = slice(c * NP, (c + 1) * NP)
        ss = slice(c * NS, (c + 1) * NS)
        m = {
            "x_p": f(inputs["x_prompt"][ps]), "x_s": f(inputs["x_sample"][ss]),
            "c_ckv": f(inputs["cache_ckv"][0, ss]), "c_kr": f(inputs["cache_krope"][0, ss]),
            "c_dk": f(np.asarray(inputs["cache_diff_k"])[0, ss].reshape(NS, PAST, W_B)),
            "c_dv": f(np.asarray(inputs["cache_diff_v"])[0, ss].reshape(NS, PAST, W_B)),
            "c_conv": f(inputs["state_ffn_conv"][0, ss]),
            "p_p": f(inputs["p_prompt"][0, ps]), "p_s": f(inputs["p_sample"][0, ss]),
            "w_in": f(inputs["w_in"][0]), "g_cq": f(inputs["g_cq"]), "w_uq": f(inputs["w_uq"][0]),
            "g_ckv": f(inputs["g_ckv"]), "w_uk": f(inputs["w_uk"][0]), "w_uv": f(inputs["w_uv"][0]),
            "lq1": f(inputs["lambda_q1"]), "lk1": f(inputs["lambda_k1"]), "lq2": f(inputs["lambda_q2"]), "lk2": f(inputs["lambda_k2"]),
            "g_sub": f(inputs["g_subln"]), "w_o": f(inputs["w_o"][0]), "ln1_g": f(inputs["ln1_g"]), "ln1_b": f(inputs["ln1_b"]),
            "w_up": f(inputs["w_up"][0]), "conv_w": f(inputs["conv_w"][0]), "conv_b": f(inputs["conv_b"]),
            "w_down": f(inputs["w_down"][0]), "ln2_g": f(inputs["ln2_g"]), "ln2_b": f(inputs["ln2_b"]),
            "w_gate": f(inputs["w_ple_gate"][0]), "b_gate": f(inputs["b_ple_gate"]), "w_ple": f(inputs["w_ple_proj"][0]),
        }
        m.update(tables)
        in_maps.append(m)
    res = run_bass_kernel_spmd(nc, in_maps, core_ids=list(range(ncores)))
    R = res.results
    cat = lambda k: np.concatenate([np.asarray(r[k], dtype=np.float32) for r in R], axis=0)
    B = NP * ncores
    BS = NS * ncores
    return (cat("y_p"), cat("y_s"),
            cat("ckv_p")[None], cat("kr_p")[None], cat("dk_p").reshape(1, B, S, H_B, 2, D_HB), cat("dv_p").reshape(1, B, S, H_B, 2 * D_HB),
            cat("conv_p")[None],
            cat("ckv_s")[None], cat("kr_s")[None], cat("dk_s").reshape(1, BS, TS, H_B, 2, D_HB), cat("dv_s").reshape(1, BS, TS, H_B, 2 * D_HB),
            cat("conv_s")[None])


def kernel(**inputs):
    return run(inputs, 4, 2048, 4, 2048, 8)
```

```python
import math
import numpy as np
import concourse.bass as bass
import concourse.mybir as mybir
from concourse.bass_utils import run_bass_kernel_spmd

F32 = mybir.dt.float32
BF16 = mybir.dt.bfloat16
AF = mybir.ActivationFunctionType
ALU = mybir.AluOpType

D = 1024
H_A, D_NOPE, D_ROPE, D_VA, D_CQ, D_CKV = 8, 64, 32, 64, 256, 256
H_B, D_HB = 4, 64
W_B = 512
D_IN = 2080
D_FF = 2816
NCH = 22
D_PLE = 256
ALPHA = 2.0 ** 0.25
EPS = 1e-5
LAM_INIT = 0.8 - 0.6 * math.exp(0.0)
TS = 32
T = 256
SLOPES = [2.0 ** (-8.0 * (h + 1) / H_B) for h in range(H_B)]
SC_A = (D_NOPE + D_ROPE) ** -0.5
SC_B = D_HB ** -0.5
NEG = -1.0e5

import os
SAME_ENGINE_SYNC = os.environ.get('MK_SES', '1') == '1'


class Buf:
    __slots__ = ("name", "w", "r", "dsem", "dcount", "excl")

    def __init__(self, name, excl=False):
        self.name = name
        self.w = None
        self.r = {}
        self.dsem = None
        self.dcount = 0
        self.excl = excl


class FW:
    ENG = ("pe", "act", "dve", "pool", "sp")

    def __init__(self, nc):
        self.nc = nc
        self.streams = {e: [] for e in self.ENG}
        self.tick = {e: 0 for e in self.ENG}
        self.known = {e: {} for e in self.ENG}
        self.sems = {}
        self.out_tokens = []
        self.dsem_pool = []
        for e in ("pe", "act", "dve", "pool"):
            self._sem("eng_" + e)

    def _sem(self, key):
        if key not in self.sems:
            self.sems[key] = self.nc.alloc_semaphore("s_" + key)
        return key

    def _deps(self, eng, reads, writes):
        deps = {}

        def add(tok):
            if tok is None:
                return
            k, v = tok
            if deps.get(k, 0) < v:
                deps[k] = v
        writes = list(writes)
        for b in reads:
            if b.excl:
                writes.append(b)
                continue
            add(b.w)
        for b in writes:
            add(b.w)
            for k, v in b.r.items():
                add((k, v))
        own = "eng_" + eng
        out = []
        for k, v in deps.items():
            if k == own and (eng == "pe" or not SAME_ENGINE_SYNC):
                continue
            if self.known[eng].get(k, 0) >= v:
                continue
            self.known[eng][k] = v
            out.append((k, v))
        return out, writes

    def op(self, eng, fn, reads=(), writes=()):
        waits, writes = self._deps(eng, reads, writes)
        self.tick[eng] += 1
        key = "eng_" + eng
        tok = (key, self.tick[eng])
        self.streams[eng].append((waits, fn, (key, 1)))
        for b in reads:
            if not b.excl:
                b.r[key] = tok[1]
        for b in writes:
            b.w = tok
            b.r = {}
        return tok

    def dma(self, q, out_ap, in_ap, reads=(), writes=(), owner=None, final=False, **kw):
        if owner is None:
            owner = (list(writes) + list(reads))[0]
        if isinstance(owner, TT):
            owner = owner.buf
        if owner.dsem is None:
            owner.dsem = {}
            owner.dcount = {}
        kind = "sw" if q == "pool" else "hw"
        if kind not in owner.dsem:
            owner.dsem[kind] = self._sem("dma" + kind + "_" + owner.name)
            owner.dcount[kind] = 0
        waits, writes = self._deps(q, reads, writes)
        owner.dcount[kind] += 16
        tok = (owner.dsem[kind], owner.dcount[kind])

        def fn(engobj, out_ap=out_ap, in_ap=in_ap, kw=kw):
            return engobj.dma_start(out=out_ap, in_=in_ap, **kw)
        self.streams[q].append((waits, fn, (tok[0], 16)))
        for b in reads:
            if not b.excl:
                b.r[tok[0]] = tok[1]
        for b in writes:
            b.w = tok
            b.r = {}
        if final:
            self.out_tokens.append(tok)
        return tok

    def finish(self):
        fin = {}
        for k, v in self.out_tokens:
            fin[k] = max(fin.get(k, 0), v)
        self.streams["sp"].append((list(fin.items()), None, None))
        nc = self.nc
        sems = self.sems
        streams = self.streams

        def replay(eng, name):
            for waits, fn, inc in streams[name]:
                for k, v in waits:
                    eng.wait_ge(sems[k], v)
                if fn is not None:
                    ins = fn(eng)
                    ins.then_inc(sems[inc[0]], inc[1])

        with nc.Block() as block:
            @block.tensor
            def _(e):
                replay(e, "pe")

            @block.scalar
            def _(e):
                replay(e, "act")

            @block.vector
            def _(e):
                replay(e, "dve")

            @block.gpsimd
            def _(e):
                replay(e, "pool")

            @block.sync
            def _(e):
                replay(e, "sp")


class TT:
    def __init__(self, ap, buf):
        self.ap = ap
        self.buf = buf


def make_tables(S, PAST):
    nblk = max(S, PAST) // 128 + 1
    half = D_ROPE // 2
    inv = (1.0 / (10000.0 ** (np.arange(half, dtype=np.float32) / half))).astype(np.float32)
    pos = (np.arange(nblk * 128, dtype=np.float32)).reshape(nblk, 128)
    ang = pos[:, :, None] * inv[None, None, :]
    cos = np.cos(ang).astype(np.float32).transpose(1, 0, 2).copy()
    sin = np.sin(ang).astype(np.float32).transpose(1, 0, 2).copy()
    s = np.arange(128)[:, None]
    t = np.arange(128)[None, :]
    masked = (s // 64) > (t // 64)
    dt = np.zeros((5, 128, 128), np.float32)
    dt[0] = np.where(masked, 0.0, 1.0)
    for h in range(H_B):
        corr = np.where((s > t) & ((s // 64) == (t // 64)), np.exp(-2.0 * SLOPES[h] * (s - t)), 1.0)
        dt[1 + h] = np.where(masked, 0.0, corr)
    dtab = dt.transpose(1, 0, 2).copy()
    off = max(S, PAST) // 128
    nb = off + 2
    d = np.arange(nb) - off
    bt = np.zeros((128, H_B, nb), np.float32)
    for h in range(H_B):
        bt[:, h, :] = SLOPES[h] * (128.0 * d[None, :] + np.arange(128)[:, None])
    ident = np.eye(128, dtype=np.float32)
    return {"t_cos": cos, "t_sin": sin, "t_diag": dtab, "t_bias": bt, "t_ident": ident}, nblk, nb, off


def build(NP, S, NS, PAST):
    nc = bass.Bass("TRN2", target_bir_lowering=False)
    fw = FW(nc)
    tables, NBLK, NBIAS, BOFF = make_tables(S, PAST)

    def din(name, shape):
        return nc.dram_tensor(name, list(shape), F32, kind="ExternalInput").ap()

    def dout(name, shape):
        return nc.dram_tensor(name, list(shape), F32, kind="ExternalOutput").ap()

    I = {}
    I["x_p"] = din("x_p", [NP, S, D])
    I["x_s"] = din("x_s", [NS, TS, D])
    I["c_ckv"] = din("c_ckv", [NS, PAST, D_CKV])
    I["c_kr"] = din("c_kr", [NS, PAST, D_ROPE])
    I["c_dk"] = din("c_dk", [NS, PAST, W_B])
    I["c_dv"] = din("c_dv", [NS, PAST, W_B])
    I["c_conv"] = din("c_conv", [NS, 2, 2 * D_FF])
    I["p_p"] = din("p_p", [NP, S, D_PLE])
    I["p_s"] = din("p_s", [NS, TS, D_PLE])
    for nm, shp in [("w_in", [D, D_IN]), ("g_cq", [1, D_CQ]), ("w_uq", [D_CQ, 768]), ("g_ckv", [1, D_CKV]),
                    ("w_uk", [D_CKV, 512]), ("w_uv", [D_CKV, 512]), ("lq1", [1, 64]), ("lk1", [1, 64]),
                    ("lq2", [1, 64]), ("lk2", [1, 64]), ("g_sub", [1, 128]), ("w_o", [D, D]),
                    ("ln1_g", [1, D]), ("ln1_b", [1, D]), ("w_up", [D, 2 * D_FF]), ("conv_w", [3, 2 * D_FF]),
                    ("conv_b", [1, 2 * D_FF]), ("w_down", [D_FF, D]), ("ln2_g", [1, D]), ("ln2_b", [1, D]),
                    ("w_gate", [D, D]), ("b_gate", [1, D]), ("w_ple", [D_PLE, D])]:
        I[nm] = din(nm, shp)
    for nm, arr in tables.items():
        I[nm] = din(nm, arr.shape)
    O = {}
    O["y_p"] = dout("y_p", [NP, S, D])
    O["y_s"] = dout("y_s", [NS, TS, D])
    O["ckv_p"] = dout("ckv_p", [NP, S, D_CKV])
    O["kr_p"] = dout("kr_p", [NP, S, D_ROPE])
    O["dk_p"] = dout("dk_p", [NP, S, W_B])
    O["dv_p"] = dout("dv_p", [NP, S, W_B])
    O["conv_p"] = dout("conv_p", [NP, 2, 2 * D_FF])
    O["ckv_s"] = dout("ckv_s", [NS, TS, D_CKV])
    O["kr_s"] = dout("kr_s", [NS, TS, D_ROPE])
    O["dk_s"] = dout("dk_s", [NS, TS, W_B])
    O["dv_s"] = dout("dv_s", [NS, TS, W_B])
    O["conv_s"] = dout("conv_s", [NS, 2, 2 * D_FF])

    cnt = [0]

    def sb(name, shape, dt=F32, excl=False):
        cnt[0] += 1
        t = nc.alloc_sbuf_tensor(f"{name}_{cnt[0]}", list(shape), dt).ap()
        return TT(t, Buf(f"{name}_{cnt[0]}"))

    def dram(name, shape, dt=BF16):
        cnt[0] += 1
        t = nc.dram_tensor(f"{name}_{cnt[0]}", list(shape), dt).ap()
        return TT(t, Buf(f"{name}_{cnt[0]}"))

    banks = []
    for i in range(8):
        a = nc.alloc_psum_tensor(f"bank{i}", [128, 512], F32).ap()
        banks.append(TT(a, Buf(f"bank{i}", excl=True)))

    class Pool:
        def __init__(self, bs):
            self.bs = bs
            self.i = 0

        def get(self):
            b = self.bs[self.i % len(self.bs)]
            self.i += 1
            return b
    G = Pool(banks[0:4])
    V = Pool(banks[4:8])

    cp = [0]

    def copy(out_ap, in_ap, reads, writes, eng=None):
        if eng is None:
            cp[0] += 1
            eng = "act" if cp[0] % 2 else "dve"
        if eng == "act":
            fw.op("act", lambda e, o=out_ap, i=in_ap: e.copy(out=o, in_=i), reads=reads, writes=writes)
        else:
            fw.op(eng, lambda e, o=out_ap, i=in_ap: e.tensor_copy(out=o, in_=i), reads=reads, writes=writes)

    ident = sb("ident", [128, 128])
    fw.dma("sp", ident.ap, I["t_ident"], writes=[ident.buf])
    cos_t = sb("cos", [128, NBLK, 16])
    sin_t = sb("sin", [128, NBLK, 16])
    fw.dma("sp", cos_t.ap, I["t_cos"], writes=[cos_t.buf])
    fw.dma("sp", sin_t.ap, I["t_sin"], writes=[sin_t.buf])
    diag_t = sb("diag", [128, 5, 128])
    fw.dma("sp", diag_t.ap, I["t_diag"], writes=[diag_t.buf])
    bias_t = sb("bias", [128, H_B, NBIAS])
    fw.dma("sp", bias_t.ap, I["t_bias"], writes=[bias_t.buf])

    def bcast_load(name, src, n):
        t = sb(name, [128, n])
        fw.dma("sp", t.ap, src.partition_broadcast(128).rearrange("p o n -> p (o n)"), writes=[t.buf])
        return t
    g_cq = bcast_load("g_cq", I["g_cq"], D_CQ)
    g_ckv = bcast_load("g_ckv", I["g_ckv"], D_CKV)
    g_sub = bcast_load("g_sub", I["g_sub"], 128)
    ln1g = bcast_load("ln1g", I["ln1_g"], D)
    ln1b = bcast_load("ln1b", I["ln1_b"], D)
    ln2g = bcast_load("ln2g", I["ln2_g"], D)
    ln2b = bcast_load("ln2b", I["ln2_b"], D)
    lq1 = bcast_load("lq1", I["lq1"], 64)
    lk1 = bcast_load("lk1", I["lk1"], 64)
    lq2 = bcast_load("lq2", I["lq2"], 64)
    lk2 = bcast_load("lk2", I["lk2"], 64)
    eps_t = sb("eps", [128, 1])
    fw.op("dve", lambda e: e.memset(eps_t.ap, EPS), writes=[eps_t.buf])
    lam_s = sb("lam_s", [128, 8])
    fw.op("dve", lambda e: e.memset(lam_s.ap, 0.0), writes=[lam_s.buf])
    fw.op("dve", lambda e: e.tensor_tensor(out=lq1.ap, in0=lq1.ap, in1=lk1.ap, op=ALU.mult), reads=[lk1.buf], writes=[lq1.buf])
    fw.op("dve", lambda e: e.tensor_tensor(out=lq2.ap, in0=lq2.ap, in1=lk2.ap, op=ALU.mult), reads=[lk2.buf], writes=[lq2.buf])
    fw.op("act", lambda e: e.activation(out=lk1.ap, in_=lq1.ap, func=AF.Identity, accum_out=lam_s.ap[:, 0:1]), reads=[lq1.buf], writes=[lk1.buf, lam_s.buf])
    fw.op("act", lambda e: e.activation(out=lk2.ap, in_=lq2.ap, func=AF.Identity, accum_out=lam_s.ap[:, 1:2]), reads=[lq2.buf], writes=[lk2.buf, lam_s.buf])
    fw.op("act", lambda e: e.activation(out=lam_s.ap[:, 2:4], in_=lam_s.ap[:, 0:2], func=AF.Exp), reads=[], writes=[lam_s.buf])
    fw.op("dve", lambda e: e.scalar_tensor_tensor(out=lam_s.ap[:, 4:5], in0=lam_s.ap[:, 3:4], scalar=-LAM_INIT, in1=lam_s.ap[:, 2:3], op0=ALU.add, op1=ALU.subtract), writes=[lam_s.buf])
    nlam = lam_s.ap[:, 4:5]
    fw.op("dve", lambda e: e.tensor_scalar(out=g_sub.ap, in0=g_sub.ap, scalar1=1.0 - LAM_INIT, scalar2=None, op0=ALU.mult), writes=[g_sub.buf])
    cw = sb("cw", [128, 3, 2 * NCH])
    cb = sb("cb", [128, 2 * NCH])
    for j in range(3):
        fw.dma("sp", cw.ap[:, j, :], I["conv_w"][j:j + 1, :].rearrange("o (u p) -> p (o u)", p=128), writes=[cw.buf], allow_slow_non_contiguous=True)
    fw.dma("sp", cb.ap, I["conv_b"].rearrange("o (u p) -> p (o u)", p=128), writes=[cb.buf], allow_slow_non_contiguous=True)
    w_uq = sb("w_uq", [128, 2, 768], BF16)
    w_uk = sb("w_uk", [128, 2, 512], BF16)
    w_uv = sb("w_uv", [128, 2, 512], BF16)
    fw.dma("pool", w_uq.ap, I["w_uq"].rearrange("(kc p) n -> p kc n", p=128), writes=[w_uq.buf])
    fw.dma("pool", w_uk.ap, I["w_uk"].rearrange("(kc p) n -> p kc n", p=128), writes=[w_uk.buf])
    fw.dma("pool", w_uv.ap, I["w_uv"].rearrange("(kc p) n -> p kc n", p=128), writes=[w_uv.buf])
    bgate = sb("bgate", [1, D], BF16)
    fw.dma("pool", bgate.ap, I["b_gate"], writes=[bgate.buf])
    ones_r = sb("ones_r", [1, 128], BF16)
    fw.op("dve", lambda e: e.memset(ones_r.ap, 1.0), writes=[ones_r.buf])

    SLOT_E = 8 * 544
    NSLOT = 3
    slots = [sb(f"wslot{i}", [128, SLOT_E], BF16) for i in range(NSLOT)]
    groups = {}
    sched = []

    def conv_group(name, n, pieces):
        scr = dram("ws_" + name, [128, n])
        si = len(groups) % NSLOT
        st = slots[si]
        for (dst_fn, src_ap) in pieces:
            fw.dma("pool", dst_fn(st.ap), src_ap, writes=[st.buf])
        fw.dma("sp", scr.ap, st.ap[:, 0:n], reads=[st.buf], writes=[scr.buf], owner=st.buf)
        groups[name] = (scr, n)
        sched.append(name)

    def kview(ap, kc, n, c0=0, c1=None):
        v = ap[:, 0:kc * n].rearrange("p (k n) -> p k n", k=kc)
        return v if c1 is None else v[:, :, c0:c1]
    w_in_v = I["w_in"].rearrange("(kc p) n -> p kc n", p=128)
    conv_group("in0", 8 * 544, [(lambda a: kview(a, 8, 544), w_in_v[:, :, 0:544])])
    for nm, c0 in (("dq", 544), ("dk", 1056), ("dv", 1568)):
        conv_group(nm, 8 * 512, [(lambda a: kview(a, 8, 512), w_in_v[:, :, c0:c0 + 512])])
    w_o_v = I["w_o"].rearrange("(kc p) n -> p kc n", p=128)
    for i in range(2):
        conv_group(f"wo{i}", 8 * 512, [(lambda a: kview(a, 8, 512), w_o_v[:, :, i * 512:(i + 1) * 512])])
    w_up_v = I["w_up"].rearrange("(kc p) n -> p kc n", p=128)
    for c in range(NCH):
        conv_group(f"up{c}", 8 * 256, [
            (lambda a: kview(a, 8, 256, 0, 128), w_up_v[:, :, c * 128:(c + 1) * 128]),
            (lambda a: kview(a, 8, 256, 128, 256), w_up_v[:, :, D_FF + c * 128:D_FF + (c + 1) * 128])])
    DN = [(i * 4, min(4, NCH - i * 4)) for i in range((NCH + 3) // 4)]
    for i, (c0, ncc) in enumerate(DN):
        src = I["w_down"][c0 * 128:(c0 + ncc) * 128, :].rearrange("(ci p) n -> p ci n", p=128)
        conv_group(f"dn{i}", ncc * 1024, [(lambda a, ncc=ncc: kview(a, ncc, 1024), src)])
    w_g_v = I["w_gate"].rearrange("(kc p) n -> p kc n", p=128)
    for i in range(2):
        conv_group(f"gt{i}", 8 * 512, [(lambda a: kview(a, 8, 512), w_g_v[:, :, i * 512:(i + 1) * 512])])
    w_p_v = I["w_ple"].rearrange("(kc p) n -> p kc n", p=128)
    for i in range(2):
        conv_group(f"pl{i}", 2 * 512, [(lambda a: kview(a, 2, 512), w_p_v[:, :, i * 512:(i + 1) * 512])])

    FRONT_GROUPS = ["in0", "dk", "dv", "dq"]
    fslot = sb("wslot_front", [128, SLOT_E], BF16)

    class Feeder:
        def __init__(self, names, ring):
            self.names = names
            self.ring = ring
            self.i = 0
            self.loaded = 0

        def _load(self):
            nm = self.names[self.loaded % len(self.names)]
            scr, n = groups[nm]
            st = self.ring[self.loaded % len(self.ring)]
            fw.dma("sp", st.ap[:, 0:n], scr.ap, reads=[scr.buf], writes=[st.buf])
            self.loaded += 1

        def prefetch(self):
            if self.loaded < self.i + len(self.ring):
                self._load()

        def take(self, name):
            assert self.names[self.i % len(self.names)] == name, (self.names[self.i % len(self.names)], name)
            while self.loaded < self.i + len(self.ring):
                self._load()
            st = self.ring[self.i % len(self.ring)]
            self.i += 1
            return st
    feeder = Feeder([n_ for n_ in sched if n_ not in FRONT_GROUPS], slots)
    ffront = Feeder(FRONT_GROUPS, [fslot])

    XS = [sb("xs", [128, 2, D]) for _ in range(2)]
    T8 = sb("T8", [128, 8, T], BF16)
    XSB = [[Buf("xs00"), Buf("xs01")], [Buf("xs10"), Buf("xs11")]]
    T8b = [Buf("T80"), Buf("T81")]
    T8f = sb("T8f", [128, 8, T], BF16)
    T8fb = [Buf("T8f0"), Buf("T8f1")]
    cqT = sb("cqT", [128, 2, T], BF16)
    ckvT = sb("ckvT", [128, 2, T], BF16)
    pT = sb("pT", [128, 2, T], BF16)
    QAT = sb("QAT", [96, H_A, T], BF16)
    DQT = sb("DQT", [128, H_B, T], BF16)
    KVc = [{"KAT": sb("KATc", [96, H_A * T], BF16), "VA": sb("VAc", [128, 2 * H_A * 65], BF16),
            "DKT": sb("DKTc", [128, H_B * T], BF16), "VB": sb("VBc", [128, 2 * H_B * 129], BF16)} for _ in range(2)]
    KVpA = [{"KAT": sb("KATp", [96, 4 * T], BF16), "VA": sb("VAp", [128, 2 * 4 * 65], BF16)} for _ in range(2)]
    KVpB = [{"DKT": sb("DKTp", [128, T], BF16), "VB": sb("VBp", [128, 2 * 129], BF16)} for _ in range(2)]
    for kv in KVc:
        fw.op("pool", lambda e, a=kv["VA"].ap: e.memset(a, 1.0), writes=[kv["VA"].buf])
        fw.op("pool", lambda e, a=kv["VB"].ap: e.memset(a, 1.0), writes=[kv["VB"].buf])
    arena = sb("arena", [128, NCH * T], BF16)
    mix_f = arena.ap.bitcast(F32)[:, 0:2 * D].rearrange("p (b n) -> p b n", b=2)
    actT = arena.ap.rearrange("p (c t) -> p c t", c=NCH)
    actB = [Buf(f"actB{g_}") for g_ in range((NCH + 3) // 4)]
    junk = sb("junk", [128, D], BF16)
    PT = [sb(f"PT{i}", [128, T], BF16) for i in range(6)]
    ptc = [0]
    cq_f = [sb("cq_f", [128, D_CQ]) for _ in range(2)]
    ckv_f = [sb("ckv_f", [128, D_CKV]) for _ in range(2)]
    kr_f = [sb("kr_f", [128, D_ROPE]) for _ in range(2)]
    dq_f = [sb("dq_f", [128, W_B]) for _ in range(2)]
    dk_f = [sb("dk_f", [128, W_B]) for _ in range(2)]
    dv_f = [sb("dv_f", [128, W_B]) for _ in range(2)]
    q_f = [sb("q_f", [128, 768]) for _ in range(2)]
    kA_f = [sb("kA_f", [128, H_A * 96]) for _ in range(2)]
    p_f = [sb("p_f", [128, D_PLE]) for _ in range(2)]
    rtmp = sb("rtmp", [128, 4, 128])
    ob = [sb("ob", [128, 2, 128]) for _ in range(2)]
    junk2 = sb("junk2", [128, D], BF16)
    junk_f = sb("junk_f", [128, D])
    ybuf = [sb("ybuf", [128, 512]) for _ in range(2)]
    yc = [0]
    ub = [[sb("ub", [128, T + 2]) for _ in range(2)] for _ in range(2)]
    zb = [[sb("zb", [128, T]) for _ in range(2)] for _ in range(2)]
    halo = sb("halo", [128, 2 * NCH, 2])
    stats = [sb(f"st{i}", [128, 16]) for i in range(12)]
    stc = [0]

    def newstat():
        stc[0] += 1
        st = stats[stc[0] % len(stats)]
        fw.op("pool", lambda e, a=st.ap: e.memset(a, 0.0), writes=[st.buf])
        return st

    def kv_scratch(ngroups):
        return [{"KAT": dram("kvK", [96, H_A * T]), "VA": dram("kvVA", [128, 2 * H_A * 65]),
                 "DKT": dram("kvDK", [128, H_B * T]), "VB": dram("kvVB", [128, 2 * H_B * 129])} for _ in range(ngroups)]

    def tr_group(srcs, src_bufs, dst3, dst_bufs, nt, ncols):
        bk = G.get()
        n = len(srcs)
        for j, s_ap in enumerate(srcs):
            fw.op("pe", lambda e, o=bk.ap[0:ncols, j * 128:j * 128 + nt], i=s_ap: e.transpose(out=o, in_=i, identity=ident.ap[0:nt, 0:nt]),
                  reads=list(src_bufs) + [ident.buf], writes=[bk.buf])
        src3 = bk.ap[0:ncols, 0:n * 128].rearrange("p (j t) -> p j t", j=n)[:, :, 0:nt]
        copy(dst3, src3, reads=[bk.buf], writes=dst_bufs)

    def mm_tok(nt, lhs_fn, KC, rhs_fn, ncols, reads, bank=None, col0=0, extra=None):
        bk = bank if bank is not None else G.get()
        nmm = KC + (1 if extra is not None else 0)
        for kc in range(KC):
            fw.op("pe", lambda e, o=bk.ap[0:nt, col0:col0 + ncols], l=lhs_fn(kc), r=rhs_fn(kc), a=(kc == 0), z=(kc == nmm - 1):
                  e.matmul(out=o, lhsT=l, rhs=r, start=a, stop=z), reads=reads, writes=[bk.buf])
        if extra is not None:
            l, r, rb = extra
            fw.op("pe", lambda e, o=bk.ap[0:nt, col0:col0 + ncols], l=l, r=r: e.matmul(out=o, lhsT=l, rhs=r, start=False, stop=True),
                  reads=list(reads) + rb, writes=[bk.buf])
        return bk

    def rms_stat(nt, src_ap, n, src_bufs, st, col):
        fw.op("act", lambda e: e.activation(out=junk.ap[0:nt, 0:n], in_=src_ap, func=AF.Square, accum_out=st.ap[0:nt, col:col + 1]),
              reads=src_bufs, writes=[junk.buf, st.buf])
        fw.op("act", lambda e: e.activation(out=st.ap[0:nt, col + 1:col + 2], in_=st.ap[0:nt, col:col + 1], func=AF.Sqrt, bias=eps_t.ap[0:nt, :], scale=1.0 / n),
              reads=[eps_t.buf], writes=[st.buf])
        fw.op("dve", lambda e: e.reciprocal(out=st.ap[0:nt, col + 2:col + 3], in_=st.ap[0:nt, col + 1:col + 2]), writes=[st.buf])
        return st.ap[0:nt, col + 2:col + 3]

    def rope(nt, x3, bufs, blk, H):
        c = cos_t.ap[0:nt, blk, :].unsqueeze(1).to_broadcast([nt, H, 16])
        s = sin_t.ap[0:nt, blk, :].unsqueeze(1).to_broadcast([nt, H, 16])
        x1 = x3[:, :, 0:16]
        x2 = x3[:, :, 16:32]
        tv = [rtmp.ap[0:nt, i, 0:H * 16].rearrange("p (h d) -> p h d", h=H) for i in range(4)]
        rd = list(bufs) + [cos_t.buf, sin_t.buf]
        fw.op("dve", lambda e: e.tensor_tensor(out=tv[0], in0=x1, in1=c, op=ALU.mult), reads=rd, writes=[rtmp.buf])
        fw.op("dve", lambda e: e.tensor_tensor(out=tv[1], in0=x2, in1=s, op=ALU.mult), reads=rd, writes=[rtmp.buf])
        fw.op("dve", lambda e: e.tensor_tensor(out=tv[2], in0=x1, in1=s, op=ALU.mult), reads=rd, writes=[rtmp.buf])
        fw.op("dve", lambda e: e.tensor_tensor(out=tv[3], in0=x2, in1=c, op=ALU.mult), reads=rd, writes=[rtmp.buf])
        fw.op("dve", lambda e: e.tensor_tensor(out=x1, in0=tv[0], in1=tv[1], op=ALU.subtract), reads=[rtmp.buf], writes=bufs)
        fw.op("dve", lambda e: e.tensor_tensor(out=x2, in0=tv[2], in1=tv[3], op=ALU.add), reads=[rtmp.buf], writes=bufs)

    def layernorm(items, g, b):
        sts = [newstat() for _ in items]
        jk = [junk, junk2]

        def stage(fn):
            for i, (nt, x_ap, xb_) in enumerate(items):
                fn(i, nt, x_ap, sts[i], sts[i].ap, [xb_])
        stage(lambda i, nt, x, st, s, x_bufs: fw.op("act", lambda e: e.activation(out=jk[i % 2].ap[0:nt, :], in_=x, func=AF.Identity, accum_out=s[0:nt, 0:1]), reads=x_bufs, writes=[jk[i % 2].buf, st.buf]))
        stage(lambda i, nt, x, st, s, x_bufs: fw.op("act", lambda e: e.activation(out=jk[i % 2].ap[0:nt, :], in_=x, func=AF.Square, accum_out=s[0:nt, 1:2]), reads=x_bufs, writes=[jk[i % 2].buf, st.buf]))
        stage(lambda i, nt, x, st, s, x_bufs: fw.op("dve", lambda e: e.tensor_scalar(out=s[0:nt, 2:3], in0=s[0:nt, 0:1], scalar1=1.0 / D, scalar2=None, op0=ALU.mult), writes=[st.buf]))
        stage(lambda i, nt, x, st, s, x_bufs: fw.op("dve", lambda e: e.tensor_tensor(out=s[0:nt, 3:4], in0=s[0:nt, 2:3], in1=s[0:nt, 2:3], op=ALU.mult), writes=[st.buf]))
        stage(lambda i, nt, x, st, s, x_bufs: fw.op("dve", lambda e: e.scalar_tensor_tensor(out=s[0:nt, 4:5], in0=s[0:nt, 1:2], scalar=1.0 / D, in1=s[0:nt, 3:4], op0=ALU.mult, op1=ALU.subtract), writes=[st.buf]))
        stage(lambda i, nt, x, st, s, x_bufs: fw.op("act", lambda e: e.activation(out=s[0:nt, 5:6], in_=s[0:nt, 4:5], func=AF.Sqrt, bias=eps_t.ap[0:nt, :], scale=1.0), reads=[eps_t.buf], writes=[st.buf]))
        stage(lambda i, nt, x, st, s, x_bufs: fw.op("dve", lambda e: e.reciprocal(out=s[0:nt, 6:7], in_=s[0:nt, 5:6]), writes=[st.buf]))
        stage(lambda i, nt, x, st, s, x_bufs: fw.op("dve", lambda e: e.scalar_tensor_tensor(out=s[0:nt, 7:8], in0=s[0:nt, 2:3], scalar=-1.0, in1=s[0:nt, 6:7], op0=ALU.mult, op1=ALU.mult), writes=[st.buf]))
        stage(lambda i, nt, x, st, s, x_bufs: fw.op("act", lambda e: e.activation(out=x, in_=x, func=AF.Identity, bias=s[0:nt, 7:8], scale=s[0:nt, 6:7]), reads=[st.buf], writes=x_bufs))
        stage(lambda i, nt, x, st, s, x_bufs: fw.op("dve", lambda e: e.tensor_tensor(out=x, in0=x, in1=g.ap[0:nt, :], op=ALU.mult), reads=[g.buf], writes=x_bufs))
        stage(lambda i, nt, x, st, s, x_bufs: fw.op("dve", lambda e: e.tensor_tensor(out=x, in0=x, in1=b.ap[0:nt, :], op=ALU.add), reads=[b.buf], writes=x_bufs))

    def build_kv(kvc, blocks, srcs):
        KAT3 = kvc["KAT"].ap.rearrange("p (h t) -> p h t", h=H_A)
        VA4 = kvc["VA"].ap.rearrange("p (k h d) -> p k h d", k=2, h=H_A)
        DKT3 = kvc["DKT"].ap.rearrange("p (h t) -> p h t", h=H_B)
        VB4 = kvc["VB"].ap.rearrange("p (k h d) -> p k h d", k=2, h=H_B)
        B_ = list(enumerate(blocks))
        for b, (off, nt) in B_:
            ckv = srcs[b][0]
            tr_group([ckv.ap[0:nt, kc * 128:(kc + 1) * 128] for kc in range(2)], [ckv.buf], ckvT.ap[:, :, off:off + nt], [ckvT.buf], nt, 128)
            dv = srcs[b][3]
            copy(VB4[0:nt, b, :, 0:128], dv.ap[0:nt, :].rearrange("p (h d) -> p h d", h=H_B), reads=[dv.buf], writes=[kvc["VB"].buf], eng="pool")
        yield
        for b, (off, nt) in B_:
            dk = srcs[b][2]
            tr_group([dk.ap[0:nt, h * 128:(h + 1) * 128] for h in range(H_B)], [dk.buf], DKT3[:, :, off:off + nt], [kvc["DKT"].buf], nt, 128)
        yield
        for b, (off, nt) in B_:
            kA = kA_f[b]
            kr = srcs[b][1]
            kA3 = kA.ap.rearrange("p (h d) -> p h d", h=H_A)
            bk = mm_tok(nt, lambda kc: ckvT.ap[:, kc, off:off + nt], 2, lambda kc: w_uk.ap[:, kc, :], 512, [ckvT.buf, w_uk.buf])
            copy(kA3[0:nt, :, 0:64], bk.ap[0:nt, :].rearrange("p (h d) -> p h d", h=H_A), reads=[bk.buf], writes=[kA.buf])
            fw.op("pool", lambda e, o=kA3[0:nt, :, 64:96], i=kr.ap[0:nt, :].unsqueeze(1).to_broadcast([nt, H_A, 32]): e.tensor_copy(out=o, in_=i),
                  reads=[kr.buf], writes=[kA.buf])
        yield
        for b, (off, nt) in B_:
            bk = mm_tok(nt, lambda kc: ckvT.ap[:, kc, off:off + nt], 2, lambda kc: w_uv.ap[:, kc, :], 512, [ckvT.buf, w_uv.buf])
            copy(VA4[0:nt, b, :, 0:64], bk.ap[0:nt, :].rearrange("p (h d) -> p h d", h=H_A), reads=[bk.buf], writes=[kvc["VA"].buf])
        yield
        for hg in range(2):
            for b, (off, nt) in B_:
                kA3 = kA_f[b].ap.rearrange("p (h d) -> p h d", h=H_A)
                tr_group([kA3[0:nt, hg * 4 + j, :] for j in range(4)], [kA_f[b].buf], KAT3[:, hg * 4:hg * 4 + 4, off:off + nt], [kvc["KAT"].buf], nt, 96)
            yield

    def store_kv(kvc, scr):
        for k in ("KAT", "VA", "DKT", "VB"):
            fw.dma("sp", scr[k].ap, kvc[k].ap, reads=[kvc[k].buf], writes=[scr[k].buf], owner=kvc[k].buf)

    slot_hist = {"A": [None, None], "B": [None, None]}
    ldc = {"A": 0, "B": 0}

    def attn_tile(tile, kvc):
        ntok, blocks, pos0 = tile["ntok"], tile["blocks"], tile["pos0"]
        scr_groups = tile["past"]
        ngr = tile["npast"]
        passes = [("A", [0, 1, 2, 3]), ("A", [4, 5, 6, 7])] + [("B", [hb]) for hb in range(H_B)]
        units = []
        loads = []
        PS = []

        for pi, (kind, heads) in enumerate(passes):
            P = {"kind": kind, "heads": heads, "accs": None, "first": len(units), "last_idx": {}}
            PS.append(P)
            W = 65 if kind == "A" else 129

            def region(h, m, qb, P=P, kind=kind, heads=heads):
                if kind == "A":
                    i = heads.index(h)
                    return P["accs"][i // 3], (i % 3) * 130 + qb * 65
                return P["accs"][m], qb * 129
            P["region"] = region

            def make_loader(g, slot, kind=kind, heads=heads):
                def ld():
                    h0 = heads[0]
                    if kind == "A":
                        fw.dma("sp", slot["KAT"].ap, scr_groups[g]["KAT"].ap[:, h0 * T:(h0 + 4) * T], reads=[scr_groups[g]["KAT"].buf], writes=[slot["KAT"].buf])
                        fw.dma("sp", slot["VA"].ap.rearrange("p (k x) -> p k x", k=2),
                               scr_groups[g]["VA"].ap.rearrange("p (k x) -> p k x", k=2)[:, :, h0 * 65:(h0 + 4) * 65],
                               reads=[scr_groups[g]["VA"].buf], writes=[slot["VA"].buf])
                    else:
                        fw.dma("sp", slot["DKT"].ap, scr_groups[g]["DKT"].ap[:, h0 * T:(h0 + 1) * T], reads=[scr_groups[g]["DKT"].buf], writes=[slot["DKT"].buf])
                        fw.dma("sp", slot["VB"].ap.rearrange("p (k x) -> p k x", k=2),
                               scr_groups[g]["VB"].ap.rearrange("p (k x) -> p k x", k=2)[:, :, h0 * 129:(h0 + 1) * 129],
                               reads=[scr_groups[g]["VB"].buf], writes=[slot["VB"].buf])
                return ld

            def make_unit(g, cur, src, h, hl, ki, nk, n0, dg, m, kind=kind, region=region, W=W):
                st = {}

                def s1():
                    s0 = g * T + ki * 128
                    bk = G.get()
                    if kind == "A":
                        Ksrc = src["KAT"]
                        lhsT = Ksrc.ap.rearrange("p (h t) -> p h t", h=(H_A if cur else 4))[0:96, hl, ki * 128:ki * 128 + nk]
                        rhs = QAT.ap[0:96, h, n0:ntok]
                        qbuf = QAT.buf
                    else:
                        Ksrc = src["DKT"]
                        nh = H_B if cur else 1
                        lhsT = Ksrc.ap.rearrange("p (h t) -> p h t", h=nh)[m * 64:(m + 1) * 64, hl, ki * 128:ki * 128 + nk]
                        rhs = DQT.ap[m * 64:(m + 1) * 64, h, n0:ntok]
                        qbuf = DQT.buf
                    nq = ntok - n0
                    fw.op("pe", lambda e, o=bk.ap[0:nk, 0:nq], l=lhsT, r=rhs: e.matmul(out=o, lhsT=l, rhs=r, start=True, stop=True),
                          reads=[Ksrc.buf, qbuf], writes=[bk.buf])
                    pt = PT[ptc[0] % len(PT)]
                    ptc[0] += 1
                    if kind == "A":
                        fw.op("act", lambda e, o=pt.ap[0:nk, 0:nq], i=bk.ap[0:nk, 0:nq]: e.activation(out=o, in_=i, func=AF.Exp, scale=SC_A),
                              reads=[bk.buf], writes=[pt.buf])
                    else:
                        dd = (s0 - pos0) // 128 + BOFF
                        fw.op("act", lambda e, o=pt.ap[0:nk, 0:nq], i=bk.ap[0:nk, 0:nq], bb=bias_t.ap[0:nk, h, dd:dd + 1]:
                              e.activation(out=o, in_=i, func=AF.Exp, scale=SC_B, bias=bb), reads=[bk.buf, bias_t.buf], writes=[pt.buf])
                    if dg and not (kind == "A" and ntok < 128):
                        di = 0 if kind == "A" else 1 + h
                        wq = min(128, nq)
                        fw.op("pool", lambda e, o=pt.ap[0:nk, 0:wq], t=diag_t.ap[0:nk, di, 0:wq]: e.tensor_tensor(out=o, in0=o, in1=t, op=ALU.mult),
                              reads=[diag_t.buf], writes=[pt.buf])
                    st["pt"] = pt

                def s2():
                    pt = st["pt"]
                    if kind == "A":
                        Vsrc = src["VA"]
                        vv = Vsrc.ap.rearrange("p (k h d) -> p k h d", k=2, h=(H_A if cur else 4))[0:nk, ki, hl, :]
                    else:
                        Vsrc = src["VB"]
                        vv = Vsrc.ap.rearrange("p (k h d) -> p k h d", k=2, h=(H_B if cur else 1))[0:nk, ki, hl, :]
                    for qb, (qoff, qnt) in enumerate(blocks):
                        if qoff < n0:
                            continue
                        acc, c0 = region(h, m, qb)
                        fw.op("pe", lambda e, o=acc.ap[0:qnt, c0:c0 + W], l=pt.ap[0:nk, qoff - n0:qoff - n0 + qnt], r=vv:
                              e.matmul(out=o, lhsT=l, rhs=r, start=False, stop=False, skip_group_check=True),
                              reads=[pt.buf, Vsrc.buf], writes=[acc.buf])
                return s1, s2

            for g in range(ngr + 1):
                cur = (g == ngr)
                L = None
                if cur:
                    src = kvc
                    ktiles = [(i, nt, off, True) for i, (off, nt) in enumerate(blocks)]
                else:
                    si = ldc[kind] % 2
                    ldc[kind] += 1
                    src = (KVpA if kind == "A" else KVpB)[si]
                    L = {"kind": kind, "slot": si, "first": len(units), "last": None, "fn": make_loader(g, src), "done": False}
                    loads.append(L)
                    ktiles = [(0, 128, 0, False), (1, 128, 0, False)]
                for h in heads:
                    hl = h if cur else h - heads[0]
                    for (ki, nk, n0, dg) in ktiles:
                        for m in range(1 if kind == "A" else 2):
                            s1, s2 = make_unit(g, cur, src, h, hl, ki, nk, n0, dg, m)
                            units.append({"P": P, "s1": s1, "s2": s2})
                            P["last_idx"][h] = len(units) - 1
                if L is not None:
                    L["last"] = len(units) - 1
            P["end"] = len(units) - 1

        prev_last = {("A", 0): -1, ("A", 1): -1, ("B", 0): -1, ("B", 1): -1}
        for L in loads:
            L["prev_last"] = prev_last[(L["kind"], L["slot"])]
            prev_last[(L["kind"], L["slot"])] = L["last"]

        nqb = len(blocks)
        qn = blocks[0][1]

        def evac_head(P, h):
            kind, region = P["kind"], P["region"]
            st = newstat()
            if kind == "A":
                acc, c0 = region(h, 0, 0)
                av = acc.ap[0:qn, c0:c0 + nqb * 65].rearrange("p (q d) -> p q d", q=nqb)
                fw.op("dve", lambda e, o=st.ap[0:qn, 0:nqb].unsqueeze(2), i=av[:, :, 64:65]: e.reciprocal(out=o, in_=i), reads=[acc.buf], writes=[st.buf])
                fw.op("dve", lambda e, o=mix_f[0:qn, 0:nqb, h * 64:(h + 1) * 64], i=av[:, :, 0:64], r=st.ap[0:qn, 0:nqb].unsqueeze(2).to_broadcast([qn, nqb, 64]):
                      e.tensor_tensor(out=o, in0=i, in1=r, op=ALU.mult), reads=[acc.buf, st.buf], writes=[arena.buf] + actB)
            else:
                a0, _ = region(h, 0, 0)
                a1, _ = region(h, 1, 0)
                v0 = a0.ap[0:qn, 0:nqb * 129].rearrange("p (q d) -> p q d", q=nqb)
                v1 = a1.ap[0:qn, 0:nqb * 129].rearrange("p (q d) -> p q d", q=nqb)
                s = st.ap
                tmp = ob[h % 2]
                dst = mix_f[0:qn, 0:nqb, 512 + h * 128:512 + (h + 1) * 128]
                fw.op("dve", lambda e, o=s[0:qn, 0:nqb].unsqueeze(2), i=v0[:, :, 128:129]: e.reciprocal(out=o, in_=i), reads=[a0.buf], writes=[st.buf])
                fw.op("dve", lambda e, o=s[0:qn, 2:2 + nqb].unsqueeze(2), i=v1[:, :, 128:129]: e.reciprocal(out=o, in_=i), reads=[a1.buf], writes=[st.buf])
                fw.op("dve", lambda e, o=s[0:qn, 4:4 + nqb], i=s[0:qn, 2:2 + nqb], l=nlam[0:qn, :]: e.tensor_scalar(out=o, in0=i, scalar1=l, scalar2=None, op0=ALU.mult),
                      reads=[lam_s.buf], writes=[st.buf])
                fw.op("dve", lambda e, o=dst, i=v0[:, :, 0:128], r=s[0:qn, 0:nqb].unsqueeze(2).to_broadcast([qn, nqb, 128]):
                      e.tensor_tensor(out=o, in0=i, in1=r, op=ALU.mult), reads=[a0.buf, st.buf], writes=[arena.buf] + actB)
                fw.op("dve", lambda e, o=tmp.ap[0:qn, 0:nqb, :], i=v1[:, :, 0:128], r=s[0:qn, 4:4 + nqb].unsqueeze(2).to_broadcast([qn, nqb, 128]):
                      e.tensor_tensor(out=o, in0=i, in1=r, op=ALU.mult), reads=[a1.buf, st.buf], writes=[tmp.buf])
                fw.op("dve", lambda e, o=dst, t=tmp.ap[0:qn, 0:nqb, :]: e.tensor_tensor(out=o, in0=o, in1=t, op=ALU.add), reads=[tmp.buf], writes=[arena.buf] + actB)

        def subln_all():
            st = newstat()
            for qb in range(nqb):
                x4 = mix_f[0:qn, qb, 512:1024].rearrange("p (h d) -> p h d", h=H_B)
                sq = junk_f.ap[0:qn, qb * 512:(qb + 1) * 512].rearrange("p (h d) -> p h d", h=H_B)
                fw.op("dve", lambda e, o=sq, i=x4: e.tensor_tensor(out=o, in0=i, in1=i, op=ALU.mult), reads=[arena.buf], writes=[junk_f.buf])
                fw.op("dve", lambda e, o=st.ap[0:qn, qb * 4:qb * 4 + 4], i=sq: e.reduce_sum(out=o, in_=i, axis=mybir.AxisListType.X), reads=[junk_f.buf], writes=[st.buf])
            n4 = nqb * 4
            fw.op("act", lambda e: e.activation(out=st.ap[0:qn, 8:8 + n4], in_=st.ap[0:qn, 0:n4], func=AF.Sqrt, bias=eps_t.ap[0:qn, :], scale=1.0 / 128),
                  reads=[eps_t.buf], writes=[st.buf])
            fw.op("dve", lambda e: e.reciprocal(out=st.ap[0:qn, 0:n4], in_=st.ap[0:qn, 8:8 + n4]), writes=[st.buf])
            for qb in range(nqb):
                x4 = mix_f[0:qn, qb, 512:1024].rearrange("p (h d) -> p h d", h=H_B)
                fw.op("dve", lambda e, o=x4, r=st.ap[0:qn, qb * 4:qb * 4 + 4].unsqueeze(2).to_broadcast([qn, H_B, 128]):
                      e.tensor_tensor(out=o, in0=o, in1=r, op=ALU.mult), reads=[st.buf], writes=[arena.buf] + actB)
                fw.op("dve", lambda e, o=x4, gq=g_sub.ap[0:qn, :].unsqueeze(1).to_broadcast([qn, H_B, 128]):
                      e.tensor_tensor(out=o, in0=o, in1=gq, op=ALU.mult), reads=[g_sub.buf], writes=[arena.buf] + actB)

        def pass_begin(P):
            P["accs"] = [V.get() for _ in range(2)]
            for a in P["accs"]:
                fw.op("dve", lambda e, ap=a.ap: e.memset(ap, 0.0), writes=[a.buf])

        DEPTH = 4
        nU = len(units)

        def pump_loads(done_upto, need_first=None):
            for L in loads:
                if L["done"]:
                    continue
                if L["prev_last"] <= done_upto:
                    L["fn"]()
                    L["done"] = True
                else:
                    assert need_first is None or L["first"] > need_first, "load not emittable before first use"
        def run():
          for i in range(nU + DEPTH):
              if i < nU:
                  pump_loads(i - DEPTH - 1 if i - DEPTH - 1 >= -1 else -1, need_first=i)
                  units[i]["s1"]()
              j = i - DEPTH
              if j >= 0:
                  P = units[j]["P"]
                  if j == P["first"]:
                      if P["accs"] is None:
                          pass_begin(P)
                      pi_ = PS.index(P)
                      if pi_ + 1 < len(PS) and PS[pi_ + 1]["accs"] is None:
                          pass_begin(PS[pi_ + 1])
                  units[j]["s2"]()
                  pump_loads(j)
                  for h in P["heads"]:
                      if P["last_idx"][h] == j:
                          if P["kind"] == "B":
                              evac_head(P, h)
                          else:
                              hi = P["heads"].index(h)
                              if hi % 3 == 2 or hi == len(P["heads"]) - 1:
                                  for h2 in P["heads"][(hi // 3) * 3:hi + 1]:
                                      evac_head(P, h2)
          for _ in range(5):
              tile["pull"]()
          subln_all()
        return (lambda: pump_loads(-1)), run

    def front(tile):
        ntok, blocks, pos0 = tile["ntok"], tile["blocks"], tile["pos0"]
        outs = tile["out"]
        kvc = KVc[tile["idx"] % 2]
        nb = len(blocks)
        xs = XS[tile["par"]]
        xsb = XSB[tile["par"]]
        B_ = list(enumerate(blocks))
        if not tile.get("x_loaded"):
            for b, (off, nt) in B_:
                fw.dma("sp", xs.ap[0:nt, b, :], tile["x"][off:off + nt, :], writes=[xsb[b]])
            tile["x_loaded"] = True
        ffront.prefetch()
        yield
        for b, (off, nt) in B_:
            for g4 in range(2):
                tr_group([xs.ap[0:nt, b, (g4 * 4 + j) * 128:(g4 * 4 + j + 1) * 128] for j in range(4)], [xsb[b]],
                         T8f.ap[:, g4 * 4:g4 * 4 + 4, off:off + nt], [T8fb[b]], nt, 128)
            yield
        st_w = ffront.take("in0")
        wv = st_w.ap.rearrange("p (k n) -> p k n", k=8)
        sts = {}
        rr = {}
        pbk = {}
        for b, (off, nt) in B_:
            bk = mm_tok(nt, lambda kc: T8f.ap[:, kc, off:off + nt], 8, lambda kc: wv[:, kc, 0:512], 512, [T8fb[b], st_w.buf])
            copy(dq_f[b].ap[0:nt, :], bk.ap[0:nt, :], reads=[bk.buf], writes=[dq_f[b].buf])
            bk2 = mm_tok(nt, lambda kc: T8f.ap[:, kc, off:off + nt], 8, lambda kc: wv[:, kc, 512:544], 32, [T8fb[b], st_w.buf])
            copy(kr_f[b].ap[0:nt, :], bk2.ap[0:nt, 0:32], reads=[bk2.buf], writes=[kr_f[b].buf])
            yield
        ffront.prefetch()
        for b, (off, nt) in B_:
            sts[b] = newstat()
            rr[b] = (rms_stat(nt, dq_f[b].ap[0:nt, 0:256], 256, [dq_f[b].buf], sts[b], 0),
                     rms_stat(nt, dq_f[b].ap[0:nt, 256:512], 256, [dq_f[b].buf], sts[b], 3))
        yield
        for b, (off, nt) in B_:
            blk = (pos0 + off) // 128
            rope(nt, kr_f[b].ap[0:nt, :].unsqueeze(1), [kr_f[b].buf], blk, 1)
            fw.dma("sp", outs["kr"][off:off + nt, :], kr_f[b].ap[0:nt, :], reads=[kr_f[b].buf], owner=kr_f[b], final=True)
        yield
        for b, (off, nt) in B_:
            r1, r2 = rr[b]
            fw.op("dve", lambda e, o=cq_f[b].ap[0:nt, :], i=dq_f[b].ap[0:nt, 0:256], r=r1, gg=g_cq.ap[0:nt, :]: e.scalar_tensor_tensor(out=o, in0=i, scalar=r, in1=gg, op0=ALU.mult, op1=ALU.mult),
                  reads=[dq_f[b].buf, sts[b].buf, g_cq.buf], writes=[cq_f[b].buf])
            fw.op("dve", lambda e, o=ckv_f[b].ap[0:nt, :], i=dq_f[b].ap[0:nt, 256:512], r=r2, gg=g_ckv.ap[0:nt, :]: e.scalar_tensor_tensor(out=o, in0=i, scalar=r, in1=gg, op0=ALU.mult, op1=ALU.mult),
                  reads=[dq_f[b].buf, sts[b].buf, g_ckv.buf], writes=[ckv_f[b].buf])
            fw.dma("sp", outs["ckv"][off:off + nt, :], ckv_f[b].ap[0:nt, :], reads=[ckv_f[b].buf], owner=ckv_f[b], final=True)
        yield
        for nm, dst in (("dk", dk_f), ("dv", dv_f), ("dq", dq_f)):
            st_w = ffront.take(nm)
            wv = st_w.ap[:, 0:8 * 512].rearrange("p (k n) -> p k n", k=8)
            for b, (off, nt) in B_:
                bk = mm_tok(nt, lambda kc: T8f.ap[:, kc, off:off + nt], 8, lambda kc: wv[:, kc, 0:512], 512, [T8fb[b], st_w.buf])
                copy(dst[b].ap[0:nt, :], bk.ap[0:nt, :], reads=[bk.buf], writes=[dst[b].buf])
                if nm != "dq":
                    fw.dma("sp", outs[nm][off:off + nt, :], dst[b].ap[0:nt, :], reads=[dst[b].buf], owner=dst[b], final=True)
                yield
            ffront.prefetch()
            if nm == "dk":
                for b, (off, nt) in B_:
                    tr_group([cq_f[b].ap[0:nt, kc * 128:(kc + 1) * 128] for kc in range(2)], [cq_f[b].buf], cqT.ap[:, :, off:off + nt], [cqT.buf], nt, 128)
                yield
        for b, (off, nt) in B_:
            tr_group([dq_f[b].ap[0:nt, h * 128:(h + 1) * 128] for h in range(H_B)], [dq_f[b].buf], DQT.ap[:, :, off:off + nt], [DQT.buf], nt, 128)
        yield
        for b, (off, nt) in B_:
            for (c0, ncol) in ((0, 512), (512, 256)):
                bk = mm_tok(nt, lambda kc: cqT.ap[:, kc, off:off + nt], 2, lambda kc: w_uq.ap[:, kc, c0:c0 + ncol], ncol, [cqT.buf, w_uq.buf])
                copy(q_f[b].ap[0:nt, c0:c0 + ncol], bk.ap[0:nt, 0:ncol], reads=[bk.buf], writes=[q_f[b].buf])
            yield
        for b, (off, nt) in B_:
            blk = (pos0 + off) // 128
            q3 = q_f[b].ap.rearrange("p (h d) -> p h d", h=H_A)
            rope(nt, q3[0:nt, :, 64:96], [q_f[b].buf], blk, H_A)
        yield
        for _ in build_kv(kvc, blocks, [(ckv_f[b], kr_f[b], dk_f[b], dv_f[b]) for b in range(nb)]):
            yield
        if tile["store"] is not None:
            store_kv(kvc, tile["store"])
        for b, (off, nt) in B_:
            q3 = q_f[b].ap.rearrange("p (h d) -> p h d", h=H_A)
            for hg in range(2):
                tr_group([q3[0:nt, hg * 4 + j, :] for j in range(4)], [q_f[b].buf], QAT.ap[:, hg * 4:hg * 4 + 4, off:off + nt], [QAT.buf], nt, 96)
            yield
        tile["front_done"] = True

    def prepass(item):
        s_, g, scr, kidx = item
        kvc = KVc[kidx % 2]
        blocks = [(0, 128), (128, 128)]
        for b, (off, nt) in enumerate(blocks):
            r0 = g * T + off
            fw.dma("sp", ckv_f[b].ap, I["c_ckv"][s_, r0:r0 + 128, :], writes=[ckv_f[b].buf])
            fw.dma("sp", kr_f[b].ap, I["c_kr"][s_, r0:r0 + 128, :], writes=[kr_f[b].buf])
            fw.dma("sp", dk_f[b].ap, I["c_dk"][s_, r0:r0 + 128, :], writes=[dk_f[b].buf])
            fw.dma("sp", dv_f[b].ap, I["c_dv"][s_, r0:r0 + 128, :], writes=[dv_f[b].buf])
        yield
        for _ in build_kv(kvc, blocks, [(ckv_f[b], kr_f[b], dk_f[b], dv_f[b]) for b in range(2)]):
            yield
        store_kv(kvc, scr[g])
        yield

    def back(tile, pull):
        ntok, blocks, pos0 = tile["ntok"], tile["blocks"], tile["pos0"]
        psrc, outs = tile["p"], tile["out"]
        kvc = KVc[tile["idx"] % 2]
        nb = len(blocks)
        xs = XS[tile["par"]]
        xsb = XSB[tile["par"]]
        nxt = tile.get("next")
        if nxt is not None and not nxt.get("x_loaded"):
            for b, (off, nt) in enumerate(nxt["blocks"]):
                fw.dma("sp", XS[nxt["par"]].ap[0:nt, b, :], nxt["x"][off:off + nt, :], writes=[XSB[nxt["par"]][b]])
            nxt["x_loaded"] = True
        for b, (off, nt) in enumerate(blocks):
            fw.dma("sp", p_f[b].ap[0:nt, :], psrc[off:off + nt, :], writes=[p_f[b].buf])
        tile["pull"] = pull
        if "attn" not in tile:
            tile["attn"] = attn_tile(tile, kvc)
            tile["attn"][0]()
        tile["attn"][1]()
        for b, (off, nt) in enumerate(blocks):
            for g4 in range(2):
                tr_group([mix_f[0:nt, b, (g4 * 4 + j) * 128:(g4 * 4 + j + 1) * 128] for j in range(4)], [arena.buf],
                         T8.ap[:, g4 * 4:g4 * 4 + 4, off:off + nt], [T8b[b]], nt, 128)
        for i in range(2):
            st_w = feeder.take(f"wo{i}")
            wv = st_w.ap[:, 0:8 * 512].rearrange("p (k n) -> p k n", k=8)
            for b, (off, nt) in enumerate(blocks):
                bk = mm_tok(nt, lambda kc: T8.ap[:, kc, off:off + nt], 8, lambda kc: wv[:, kc, 0:512], 512, [T8b[b], st_w.buf])
                xa = xs.ap[0:nt, b, i * 512:(i + 1) * 512]
                fw.op("dve", lambda e, o=xa, p=bk.ap[0:nt, :]: e.scalar_tensor_tensor(out=o, in0=o, scalar=ALPHA, in1=p, op0=ALU.mult, op1=ALU.add),
                      reads=[bk.buf], writes=[xsb[b]])
        for _ in range(6):
            pull()
        layernorm([(nt, xs.ap[0:nt, b, :], xsb[b]) for b, (off, nt) in enumerate(blocks)], ln1g, ln1b)
        for b, (off, nt) in enumerate(blocks):
            for g4 in range(2):
                tr_group([xs.ap[0:nt, b, (g4 * 4 + j) * 128:(g4 * 4 + j + 1) * 128] for j in range(4)], [xsb[b]],
                         T8.ap[:, g4 * 4:g4 * 4 + 4, off:off + nt], [T8b[b]], nt, 128)
        W2 = ntok + 2
        if tile.get("halo0"):
            fw.op("pool", lambda e: e.memset(halo.ap, 0.0), writes=[halo.buf])
        if tile.get("conv_in") is not None:
            ci = tile["conv_in"]
            for pc in range(6):
                w_ = min(1024, 2 * D_FF - pc * 1024)
                fw.dma("sp", junk_f.ap[0:2, 0:w_], ci[:, pc * 1024:pc * 1024 + w_], writes=[junk_f.buf])
                bkh = G.get()
                nu = w_ // 128
                for ul in range(nu):
                    fw.op("pe", lambda e, o=bkh.ap[:, ul * 2:ul * 2 + 2], i=junk_f.ap[0:2, ul * 128:(ul + 1) * 128]: e.transpose(out=o, in_=i, identity=ident.ap[0:2, 0:2]),
                          reads=[junk_f.buf, ident.buf], writes=[bkh.buf])
                copy(halo.ap[:, pc * 8:pc * 8 + nu, :], bkh.ap[:, 0:nu * 2].rearrange("p (u t) -> p u t", t=2), reads=[bkh.buf], writes=[halo.buf])
        for c in range(NCH):
            st_w = feeder.take(f"up{c}")
            wv = st_w.ap[:, 0:8 * 256].rearrange("p (k n) -> p k n", k=8)
            bk = G.get()
            for hf in range(2):
                for kc in range(8):
                    fw.op("pe", lambda e, o=bk.ap[:, hf * 256:hf * 256 + ntok], l=wv[:, kc, hf * 128:(hf + 1) * 128], r=T8.ap[:, kc, 0:ntok], a=(kc == 0), z=(kc == 7):
                          e.matmul(out=o, lhsT=l, rhs=r, start=a, stop=z), reads=[st_w.buf] + T8b[0:nb], writes=[bk.buf])
            for hf in range(2):
                ui = c + hf * NCH
                u = ub[c % 2][hf]
                z = zb[c % 2][hf]
                ucol = u.ap[:, 0:W2]
                zc = z.ap[:, 0:ntok]
                fw.op("pool", lambda e, o=ucol[:, 0:2], i=halo.ap[:, ui, :]: e.tensor_copy(out=o, in_=i), reads=[halo.buf], writes=[u.buf])
                fw.op("act", lambda e, o=ucol[:, 2:W2], i=bk.ap[:, hf * 256:hf * 256 + ntok]: e.copy(out=o, in_=i), reads=[bk.buf], writes=[u.buf])
                fw.op("pool", lambda e, o=halo.ap[:, ui, :], i=ucol[:, ntok:W2]: e.tensor_copy(out=o, in_=i), reads=[u.buf], writes=[halo.buf])
                wj = [cw.ap[:, j, ui:ui + 1] for j in range(3)]
                fw.op("act", lambda e, o=zc, i=bk.ap[:, hf * 256:hf * 256 + ntok], w=wj[2], bb=cb.ap[:, ui:ui + 1]:
                      e.activation(out=o, in_=i, func=AF.Identity, bias=bb, scale=w), reads=[bk.buf, cw.buf, cb.buf], writes=[z.buf])
                fw.op("dve", lambda e, o=zc, i=ucol[:, 1:W2 - 1], w=wj[1]:
                      e.scalar_tensor_tensor(out=o, in0=i, scalar=w, in1=o, op0=ALU.mult, op1=ALU.add), reads=[u.buf, cw.buf], writes=[z.buf])
                fw.op("dve", lambda e, o=zc, i=ucol[:, 0:W2 - 2], w=wj[0]:
                      e.scalar_tensor_tensor(out=o, in0=i, scalar=w, in1=o, op0=ALU.mult, op1=ALU.add), reads=[u.buf, cw.buf], writes=[z.buf])
            zg_t, zv_t = zb[c % 2]
            zg = zg_t.ap[:, 0:ntok]
            fw.op("act", lambda e, o=zg: e.activation(out=o, in_=o, func=AF.Gelu), writes=[zg_t.buf])
            fw.op("dve", lambda e, o=actT[:, c, 0:ntok], a=zg, b2=zv_t.ap[:, 0:ntok]: e.tensor_tensor(out=o, in0=a, in1=b2, op=ALU.mult),
                  reads=[zg_t.buf, zv_t.buf], writes=[arena.buf, actB[c // 4]])
            pull()
            if c % 2 == 1:
                pull()
        if tile["conv_out"] is not None:
            co = tile["conv_out"]
            for pc in range(6):
                w_ = min(1024, 2 * D_FF - pc * 1024)
                nu = w_ // 128
                for hb2 in range((nu + 3) // 4):
                    bkh = G.get()
                    n4 = min(4, nu - hb2 * 4)
                    for ul in range(n4):
                        u_ = pc * 8 + hb2 * 4 + ul
                        fw.op("pe", lambda e, o=bkh.ap[0:2, ul * 128:(ul + 1) * 128], i=halo.ap[:, u_, :]: e.transpose(out=o, in_=i, identity=ident.ap),
                              reads=[halo.buf, ident.buf], writes=[bkh.buf])
                    copy(junk_f.ap[0:2, hb2 * 512:hb2 * 512 + n4 * 128], bkh.ap[0:2, 0:n4 * 128], reads=[bkh.buf], writes=[junk_f.buf])
                fw.dma("sp", co[:, pc * 1024:pc * 1024 + w_], junk_f.ap[0:2, 0:w_], reads=[junk_f.buf], owner=junk_f, final=True)
        accs = [[V.get() for _ in range(2)] for _ in range(nb)]
        for gi, (c0, ncc) in enumerate(DN):
            pull()
            st_w = feeder.take(f"dn{gi}")
            wv = st_w.ap[:, 0:ncc * 1024].rearrange("p (c n) -> p c n", c=ncc)
            for b, (off, nt) in enumerate(blocks):
                for hf in range(2):
                    for ci in range(ncc):
                        c = c0 + ci
                        fw.op("pe", lambda e, o=accs[b][hf].ap[0:nt, :], l=actT[:, c, off:off + nt], r=wv[:, ci, hf * 512:(hf + 1) * 512], a=(c == 0), z=(c == NCH - 1):
                              e.matmul(out=o, lhsT=l, rhs=r, start=a, stop=z), reads=[actB[gi], st_w.buf], writes=[accs[b][hf].buf])
        for b, (off, nt) in enumerate(blocks):
            for hf in range(2):
                xa = xs.ap[0:nt, b, hf * 512:(hf + 1) * 512]
                fw.op("dve", lambda e, o=xa, p=accs[b][hf].ap[0:nt, :]: e.scalar_tensor_tensor(out=o, in0=o, scalar=ALPHA, in1=p, op0=ALU.mult, op1=ALU.add),
                      reads=[accs[b][hf].buf], writes=[xsb[b]])
        for _ in range(4):
            pull()
        layernorm([(nt, xs.ap[0:nt, b, :], xsb[b]) for b, (off, nt) in enumerate(blocks)], ln2g, ln2b)
        if nxt is not None and nxt.get("conv_in") is None and nxt.get("front_done"):
            nxt["attn"] = attn_tile(nxt, KVc[nxt["idx"] % 2])
            nxt["attn"][0]()
        for b, (off, nt) in enumerate(blocks):
            for g4 in range(2):
                tr_group([xs.ap[0:nt, b, (g4 * 4 + j) * 128:(g4 * 4 + j + 1) * 128] for j in range(4)], [xsb[b]],
                         T8.ap[:, g4 * 4:g4 * 4 + 4, off:off + nt], [T8b[b]], nt, 128)
            tr_group([p_f[b].ap[0:nt, kc * 128:(kc + 1) * 128] for kc in range(2)], [p_f[b].buf], pT.ap[:, :, off:off + nt], [pT.buf], nt, 128)
        gts = {}
        for i in range(2):
            st_w = feeder.take(f"gt{i}")
            wv = st_w.ap[:, 0:8 * 512].rearrange("p (k n) -> p k n", k=8)
            for b, (off, nt) in enumerate(blocks):
                bk = mm_tok(nt, lambda kc: T8.ap[:, kc, off:off + nt], 8, lambda kc: wv[:, kc, 0:512], 512, [T8b[b], st_w.buf],
                            extra=(ones_r.ap[0:1, 0:nt], bgate.ap[0:1, i * 512:(i + 1) * 512], [ones_r.buf, bgate.buf]))
                gt = gate_all[(i * 2 + b) % 4]
                fw.op("act", lambda e, o=gt.ap[0:nt, :], p=bk.ap[0:nt, :]: e.activation(out=o, in_=p, func=AF.Sigmoid), reads=[bk.buf], writes=[gt.buf])
                gts[(i, b)] = gt
        for i in range(2):
            st_w = feeder.take(f"pl{i}")
            wv = st_w.ap[:, 0:1024].rearrange("p (k n) -> p k n", k=2)
            for b, (off, nt) in enumerate(blocks):
                bk = mm_tok(nt, lambda kc: pT.ap[:, kc, off:off + nt], 2, lambda kc: wv[:, kc, 0:512], 512, [pT.buf, st_w.buf])
                gt = gts[(i, b)]
                yb = ybuf[yc[0] % len(ybuf)]
                yc[0] += 1
                fw.op("dve", lambda e, o=gt.ap[0:nt, :], p=bk.ap[0:nt, :]: e.tensor_tensor(out=o, in0=o, in1=p, op=ALU.mult), reads=[bk.buf], writes=[gt.buf])
                fw.op("dve", lambda e, o=yb.ap[0:nt, :], a=gt.ap[0:nt, :], h2=xs.ap[0:nt, b, i * 512:(i + 1) * 512]: e.tensor_tensor(out=o, in0=a, in1=h2, op=ALU.add),
                      reads=[gt.buf, xsb[b]], writes=[yb.buf])
                fw.dma("sp", outs["y"][off:off + nt, i * 512:(i + 1) * 512], yb.ap[0:nt, :], reads=[yb.buf], owner=yb, final=True)

    gate_all = [sb("gate_all", [128, 512]) for _ in range(4)]

    work = []
    for s_ in range(NP):
        scr = kv_scratch(S // T)
        for j in range(S // T):
            tile = {"ntok": T, "blocks": [(0, 128), (128, 128)], "pos0": j * T,
                    "x": I["x_p"][s_, j * T:(j + 1) * T, :], "p": I["p_p"][s_, j * T:(j + 1) * T, :],
                    "out": {"y": O["y_p"][s_, j * T:(j + 1) * T, :], "ckv": O["ckv_p"][s_, j * T:(j + 1) * T, :],
                            "kr": O["kr_p"][s_, j * T:(j + 1) * T, :], "dk": O["dk_p"][s_, j * T:(j + 1) * T, :],
                            "dv": O["dv_p"][s_, j * T:(j + 1) * T, :]},
                    "npast": j, "past": scr, "store": scr[j] if j < S // T - 1 else None,
                    "conv_out": O["conv_p"][s_] if j == S // T - 1 else None, "halo0": (j == 0)}
            work.append(("tile", tile))
    for s_ in range(NS):
        ng = PAST // T
        scr = kv_scratch(ng)
        for g in range(ng):
            work.append(("pre", (s_, g, scr)))
        tile = {"ntok": TS, "blocks": [(0, TS)], "pos0": PAST,
                "x": I["x_s"][s_], "p": I["p_s"][s_],
                "out": {"y": O["y_s"][s_], "ckv": O["ckv_s"][s_], "kr": O["kr_s"][s_], "dk": O["dk_s"][s_], "dv": O["dv_s"][s_]},
                "npast": ng, "past": scr, "store": None, "conv_out": O["conv_s"][s_], "conv_in": I["c_conv"][s_]}
        work.append(("tile", tile))
    tiles = [w[1] for w in work if w[0] == "tile"]
    for i, t_ in enumerate(tiles):
        t_["par"] = i % 2
        t_["next"] = tiles[i + 1] if i + 1 < len(tiles) else None
    n_prompt_tiles = NP * (S // T)
    bg_items = [it for k_, it in work if k_ == "pre"][:max(0, n_prompt_tiles - 1)]
    bg_set = set(id(it) for it in bg_items)
    work = [(k_, it) for k_, it in work if not (k_ == "pre" and id(it) in bg_set)]
    bg_state = {"i": 0}
    for wi, (kind_, item) in enumerate(work):
        if kind_ == "tile":
            item["idx"] = wi
        else:
            work[wi] = (kind_, item + (wi,))

    def exhaust(gen):
        for _ in gen:
            pass

    def gen_for(kind_, item):
        return front(item) if kind_ == "tile" else prepass(item)

    import itertools
    wi = 0
    while True:
        kind_, item = work[wi]
        exhaust(gen_for(kind_, item))
        if kind_ == "tile":
            break
        wi += 1
    while wi < len(work):
        tile = work[wi][1]
        nxt_items = []
        wj = wi + 1
        while wj < len(work):
            nxt_items.append(work[wj])
            if work[wj][0] == "tile":
                break
            wj += 1
        gens = [gen_for(k_, it_) for k_, it_ in nxt_items]
        if tile.get("conv_in") is None and bg_state["i"] < len(bg_items) and nxt_items and nxt_items[-1][0] == "tile":
            gens.append(prepass(bg_items[bg_state["i"]] + (tile["idx"],)))
            bg_state["i"] += 1
        pending = itertools.chain.from_iterable(gens)
        state = {"live": True}

        def pull(pending=pending, state=state):
            if state["live"]:
                try:
                    next(pending)
                except StopIteration:
                    state["live"] = False
        back(tile, pull)
        while state["live"]:
            pull()
        wi = wj

    fw.finish()
    return nc, tables


_CACHE = {}


def run(inputs, NP, S, NS, PAST, ncores):
    key = (NP, S, NS, PAST)
    if key not in _CACHE:
        _CACHE[key] = build(NP, S, NS, PAST)
    nc, tables = _CACHE[key]
    f = lambda a: np.ascontiguousarray(np.asarray(a, dtype=np.float32))
    in_maps = []
    for c in range(ncores):
        ps = slice(c * NP, (c + 1) * NP)
        ss = slice(c * NS, (c + 1) * NS)
        m = {
            "x_p": f(inputs["x_prompt"][ps]), "x_s": f(inputs["x_sample"][ss]),
            "c_ckv": f(inputs["cache_ckv"][0, ss]), "c_kr": f(inputs["cache_krope"][0, ss]),
            "c_dk": f(np.asarray(inputs["cache_diff_k"])[0, ss].reshape(NS, PAST, W_B)),
            "c_dv": f(np.asarray(inputs["cache_diff_v"])[0, ss].reshape(NS, PAST, W_B)),
            "c_conv": f(inputs["state_ffn_conv"][0, ss]),
            "p_p": f(inputs["p_prompt"][0, ps]), "p_s": f(inputs["p_sample"][0, ss]),
            "w_in": f(inputs["w_in"][0]), "g_cq": f(inputs["g_cq"]), "w_uq": f(inputs["w_uq"][0]),
            "g_ckv": f(inputs["g_ckv"]), "w_uk": f(inputs["w_uk"][0]), "w_uv": f(inputs["w_uv"][0]),
            "lq1": f(inputs["lambda_q1"]), "lk1": f(inputs["lambda_k1"]), "lq2": f(inputs["lambda_q2"]), "lk2": f(inputs["lambda_k2"]),
            "g_sub": f(inputs["g_subln"]), "w_o": f(inputs["w_o"][0]), "ln1_g": f(inputs["ln1_g"]), "ln1_b": f(inputs["ln1_b"]),
            "w_up": f(inputs["w_up"][0]), "conv_w": f(inputs["conv_w"][0]), "conv_b": f(inputs["conv_b"]),
            "w_down": f(inputs["w_down"][0]), "ln2_g": f(inputs["ln2_g"]), "ln2_b": f(inputs["ln2_b"]),
            "w_gate": f(inputs["w_ple_gate"][0]), "b_gate": f(inputs["b_ple_gate"]), "w_ple": f(inputs["w_ple_proj"][0]),
        }
        m.update(tables)
        in_maps.append(m)
    res = run_bass_kernel_spmd(nc, in_maps, core_ids=list(range(ncores)))
    R = res.results
    cat = lambda k: np.concatenate([np.asarray(r[k], dtype=np.float32) for r in R], axis=0)
    B = NP * ncores
    BS = NS * ncores
    return (cat("y_p"), cat("y_s"),
            cat("ckv_p")[None], cat("kr_p")[None], cat("dk_p").reshape(1, B, S, H_B, 2, D_HB), cat("dv_p").reshape(1, B, S, H_B, 2 * D_HB),
            cat("conv_p")[None],
            cat("ckv_s")[None], cat("kr_s")[None], cat("dk_s").reshape(1, BS, TS, H_B, 2, D_HB), cat("dv_s").reshape(1, BS, TS, H_B, 2 * D_HB),
            cat("conv_s")[None])


def kernel(**inputs):
    return run(inputs, 4, 2048, 4, 2048, 8)
```
